# Optimizing a Trainium2 kernel written in Bass

```python
import jax
import jax.numpy as jnp
from jax import lax
import numpy as np

D_MODEL = 2048
BATCH = 16
SEQ = 2048
DEPTH = 2

HEAD_DIM = 64
NORM_EPS = 1e-6
FFN_HIDDEN = ((8 * D_MODEL + 3 * 256 - 1) // (3 * 256)) * 256

MOBA_HEADS = D_MODEL // (2 * HEAD_DIM)
MOBA_BLOCK = 256
MOBA_TOPK = 3
MOBA_Q_CHUNK = 16
RWKV_HEADS = D_MODEL // (2 * HEAD_DIM)
RWKV_DECAY_LORA = 64
RWKV_ICLR_LORA = 64
RWKV_GATE_LORA = 128
RWKV_GN_EPS = 6.4e-4
MOBA_W = MOBA_HEADS * HEAD_DIM
RWKV_W = RWKV_HEADS * HEAD_DIM
RWKV_SPLITS = (RWKV_W, RWKV_W, RWKV_W, RWKV_DECAY_LORA, RWKV_ICLR_LORA, RWKV_GATE_LORA)
RWKV_SHIFT_W = sum(RWKV_SPLITS)
EVEN_SPLITS = (MOBA_W, MOBA_W, MOBA_W, RWKV_SHIFT_W)
EVEN_IN = sum(EVEN_SPLITS)

RET_QK_DIM = HEAD_DIM
RET_V_DIM = 2 * HEAD_DIM
RET_HEADS = D_MODEL // (2 * RET_V_DIM)
RET_CHUNK = 128
RET_GN_EPS = 1e-6
ROPE_BASE = 10000.0
RET_W = RET_HEADS * RET_V_DIM
NSA_HEADS = D_MODEL // (2 * HEAD_DIM)
NSA_KV_GROUPS = 4
NSA_CMP_BLOCK = 32
NSA_CMP_STRIDE = 16
NSA_CMP_HIDDEN = 128
NSA_SLC_BLOCK = 64
NSA_SLC_TOPN = 16
NSA_WINDOW = 512
NSA_Q_CHUNK = 32
WIN_Q_BLOCK = 128
NSA_W = NSA_HEADS * HEAD_DIM
NSA_KV_W = NSA_KV_GROUPS * HEAD_DIM
ODD_SPLITS = (RET_HEADS * RET_QK_DIM, RET_HEADS * RET_QK_DIM, RET_W, RET_W, NSA_W) + (NSA_KV_W,) * 6 + (3 * NSA_HEADS,)
ODD_IN = sum(ODD_SPLITS)
N_EVEN = (DEPTH + 1) // 2
N_ODD = DEPTH // 2

kernel_name = 'moba_rwkv7_retnet_nsa_hybrid'


def split_cols(z, sizes):
    return jnp.split(z, [int(c) for c in np.cumsum(sizes)[:-1]], axis=-1)


def rms_norm(x, g):
    xf = x.astype(jnp.float32)
    y = xf * lax.rsqrt(jnp.mean(xf * xf, axis=-1, keepdims=True) + NORM_EPS)
    return (y * g.astype(jnp.float32)).astype(x.dtype)


def head_group_norm(y, eps):
    yf = y.astype(jnp.float32)
    mu = jnp.mean(yf, axis=-1, keepdims=True)
    var = jnp.mean(jnp.square(yf - mu), axis=-1, keepdims=True)
    return (yf - mu) * lax.rsqrt(var + eps)


def masked_softmax(s, mask):
    s = jnp.where(mask, s.astype(jnp.float32), -jnp.inf)
    m = jnp.max(s, axis=-1, keepdims=True)
    p = jnp.exp(s - jnp.where(jnp.isfinite(m), m, 0.0))
    den = jnp.sum(p, axis=-1, keepdims=True)
    return p / jnp.where(den > 0.0, den, 1.0)


def token_shift(z):
    return jnp.pad(z[:, :-1], ((0, 0), (1, 0), (0, 0)))


def swiglu(h, w1, w3, w2):
    return (jax.nn.silu(h @ w1) * (h @ w3)) @ w2


def rotary(z):
    S, d = z.shape[1], z.shape[-1]
    half = d // 2
    inv = ROPE_BASE ** (-jnp.arange(half, dtype=jnp.float32) / half)
    ang = jnp.arange(S, dtype=jnp.float32)[:, None] * inv
    cos, sin = jnp.cos(ang)[None, :, None, :], jnp.sin(ang)[None, :, None, :]
    z1, z2 = z[..., :half], z[..., half:]
    return jnp.concatenate([z1 * cos - z2 * sin, z1 * sin + z2 * cos], axis=-1)


def moba_attention(q, k, v):
    B, S, H, Dh = q.shape
    L, Qc = MOBA_BLOCK, MOBA_Q_CHUNK
    nb = -(-S // L)
    n_sel = min(MOBA_TOPK, nb - 1)
    nq = S // Qc
    pad = nb * L - S
    qh = q.transpose(0, 2, 1, 3) * (Dh ** -0.5)
    pad_blocks = lambda z: jnp.pad(z.transpose(0, 2, 1, 3), ((0, 0), (0, 0), (0, pad), (0, 0))).reshape(B, H, nb, L, Dh)
    kb, vb = pad_blocks(k), pad_blocks(v)
    to_chunks = lambda z: jnp.moveaxis(z.reshape(B, H, nq, Qc, *z.shape[3:]), 2, 0)
    xs = [jnp.arange(nq), to_chunks(qh)]
    if n_sel > 0:
        t = jnp.arange(S)
        gate = jnp.einsum('bhtd,bhnd->bhtn', qh, jnp.mean(kb, axis=3)).astype(jnp.float32)
        past = jnp.arange(nb)[None, :] < (t // L)[:, None]
        sel_score, sel_idx = lax.top_k(jnp.where(past, gate, -jnp.inf), n_sel)
        xs += [to_chunks(sel_idx), to_chunks(sel_score > -jnp.inf)]
    bi = jnp.arange(B)[:, None, None, None]
    hi = jnp.arange(H)[None, :, None, None]

    def chunk(args):
        c, q_c = args[0], args[1]
        t_c = c * Qc + jnp.arange(Qc)
        blk = (c * Qc) // L
        k_own = lax.dynamic_index_in_dim(kb, blk, axis=2, keepdims=False)
        v_own = lax.dynamic_index_in_dim(vb, blk, axis=2, keepdims=False)
        scores = [jnp.einsum('bhqd,bhld->bhql', q_c, k_own)]
        masks = [jnp.broadcast_to(blk * L + jnp.arange(L) <= t_c[:, None], (B, H, Qc, L))]
        if n_sel > 0:
            idx_c, ok_c = args[2], args[3]
            k_sel = kb[bi, hi, idx_c]
            v_sel = vb[bi, hi, idx_c]
            scores.append(jnp.einsum('bhqd,bhqnld->bhqnl', q_c, k_sel).reshape(B, H, Qc, n_sel * L))
            masks.append(jnp.repeat(ok_c, L, axis=-1))
        p = masked_softmax(jnp.concatenate(scores, axis=-1), jnp.concatenate(masks, axis=-1))
        out = jnp.einsum('bhql,bhld->bhqd', p[..., :L], v_own)
        if n_sel > 0:
            out = out + jnp.einsum('bhqnl,bhqnld->bhqd', p[..., L:].reshape(B, H, Qc, n_sel, L), v_sel)
        return out

    o = lax.map(chunk, tuple(xs))
    o = jnp.moveaxis(o, 0, 2).reshape(B, H, S, Dh)
    return o.transpose(0, 2, 1, 3).reshape(B, S, H * Dh)


def rwkv7_time_mix(r, k, v, w_lo, a_lo, g_lo, w0, w_up, a0, a_up, g_up, k_k, k_a, r_k, ln_g, ln_b):
    B, S, C = r.shape
    H, N = RWKV_HEADS, C // RWKV_HEADS
    f32 = jnp.float32
    w = -jax.nn.softplus(-(w0 + jnp.tanh(w_lo) @ w_up)) - 0.5
    decay = jnp.exp(-jnp.exp(w.astype(f32)))
    a = jax.nn.sigmoid(a0 + a_lo @ a_up)
    g = jax.nn.sigmoid(g_lo) @ g_up
    hd = lambda u: u.reshape(B, S, H, N).astype(f32)
    kk = hd(k * k_k)
    kk = kk / jnp.maximum(jnp.sqrt(jnp.sum(kk * kk, axis=-1, keepdims=True)), 1e-12)
    k = k * (1.0 + (a - 1.0) * k_a)
    rh, kh, vh, ah, wh = hd(r), hd(k), hd(v), hd(a), hd(decay)

    def step(state, inp):
        r_t, w_t, k_t, v_t, kk_t, a_t = inp
        sa = jnp.einsum('bhij,bhj->bhi', state, -kk_t)
        state = (state * w_t[:, :, None, :]
                 + sa[..., None] * (kk_t * a_t)[:, :, None, :]
                 + v_t[..., None] * k_t[:, :, None, :])
        return state, jnp.einsum('bhij,bhj->bhi', state, r_t)

    xs = tuple(jnp.moveaxis(u, 1, 0) for u in (rh, wh, kh, vh, kk, ah))
    _, y = lax.scan(step, jnp.zeros((B, H, N, N), f32), xs)
    y = jnp.moveaxis(y, 0, 1)
    y = head_group_norm(y, RWKV_GN_EPS) * ln_g.reshape(H, N) + ln_b.reshape(H, N)
    bonus = jnp.sum(rh * kh * r_k, axis=-1, keepdims=True) * vh
    return ((y + bonus).reshape(B, S, C) * g).astype(r.dtype)


def retention(q, k, v):
    B, S, H, dk = q.shape
    dv = v.shape[-1]
    C = RET_CHUNK
    nc = S // C
    f32 = jnp.float32
    q = rotary(q.astype(f32))
    k = rotary(k.astype(f32)) * (dk ** -0.5)
    v = v.astype(f32)
    log_g = jnp.asarray(np.log(1.0 - 2.0 ** (-5.0 - np.arange(H))), f32)
    n = jnp.arange(C, dtype=f32)
    diff = n[:, None] - n[None, :]
    decay_in = jnp.where(diff >= 0, jnp.exp(jnp.maximum(diff, 0.0) * log_g[:, None, None]), 0.0)
    decay_q = jnp.exp((n + 1.0) * log_g[:, None])[None, :, :, None]
    decay_k = jnp.exp((C - 1.0 - n) * log_g[:, None])[None, :, :, None]
    decay_c = jnp.exp(C * log_g)[None, :, None, None]
    chunks = lambda z: z.reshape(B, nc, C, H, z.shape[-1]).transpose(1, 0, 3, 2, 4)

    def step(state, inp):
        qi, ki, vi = inp
        inner = jnp.einsum('bhnd,bhmd->bhnm', qi, ki) * decay_in
        o = jnp.einsum('bhnm,bhme->bhne', inner, vi) + jnp.einsum('bhnd,bhde->bhne', qi, state) * decay_q
        state = jnp.einsum('bhmd,bhme->bhde', ki * decay_k, vi) + decay_c * state
        return state, o

    _, o = lax.scan(step, jnp.zeros((B, H, dk, dv), f32), (chunks(q), chunks(k), chunks(v)))
    o = o.transpose(1, 0, 3, 2, 4).reshape(B, S, H, dv)
    return head_group_norm(o, RET_GN_EPS)


def nsa_attention(q, k_cmp_in, v_cmp_in, k_slc, v_slc, k_win, v_win, gates, pe_k, w1_k, w2_k, pe_v, w1_v, w2_v):
    B, S, H, Dh = q.shape
    G = NSA_KV_GROUPS
    R = H // G
    Lc, st, Ls, W = NSA_CMP_BLOCK, NSA_CMP_STRIDE, NSA_SLC_BLOCK, NSA_WINDOW
    t = jnp.arange(S)
    qg = (q * (Dh ** -0.5)).reshape(B, S, G, R, Dh).transpose(0, 2, 3, 1, 4)
    kv = lambda z: z.transpose(0, 2, 1, 3)

    nc = (S - Lc) // st + 1
    c_start = np.arange(nc) * st
    c_idx = c_start[:, None] + np.arange(Lc)[None, :]

    def compress(z, pe, w1, w2):
        blocks = kv(z)[:, :, c_idx] + pe
        return jax.nn.gelu(blocks.reshape(B, G, nc, Lc * Dh) @ w1) @ w2

    k_c = compress(k_cmp_in, pe_k, w1_k, w2_k)
    v_c = compress(v_cmp_in, pe_v, w1_v, w2_v)
    c_end = jnp.asarray(c_start + Lc - 1)
    p_cmp = masked_softmax(jnp.einsum('bgrtd,bgcd->bgrtc', qg, k_c), c_end[None, :] <= t[:, None])
    o_cmp = jnp.einsum('bgrtc,bgcd->bgrtd', p_cmp, v_c)

    ns = S // Ls
    s_start = np.arange(ns) * Ls
    overlap = ((c_start[:, None] <= s_start[None, :] + Ls - 1)
               & (c_start[:, None] + Lc - 1 >= s_start[None, :])).astype(np.float32)
    p_slc = jnp.einsum('bgrtc,cj->bgtj', p_cmp, jnp.asarray(overlap))
    j = jnp.arange(ns)[None, :]
    own = (t // Ls)[:, None]
    score = jnp.where((j == own) | (j == 0), jnp.inf, jnp.where(j > own, -jnp.inf, p_slc))
    n_top = min(NSA_SLC_TOPN, ns)
    sel_score, sel_idx = lax.top_k(score, n_top)
    sel_ok = sel_score > -jnp.inf
    ks_b = kv(k_slc).reshape(B, G, ns, Ls, Dh)
    vs_b = kv(v_slc).reshape(B, G, ns, Ls, Dh)
    Qc = NSA_Q_CHUNK
    nq = S // Qc
    bi = jnp.arange(B)[:, None, None, None]
    gi = jnp.arange(G)[None, :, None, None]

    def sel_chunk(args):
        c, q_c, idx_c, ok_c = args
        t_c = c * Qc + jnp.arange(Qc)
        k_s = ks_b[bi, gi, idx_c]
        v_s = vs_b[bi, gi, idx_c]
        pos = idx_c[..., None] * Ls + jnp.arange(Ls)
        mask = ok_c[..., None] & (pos <= t_c[None, None, :, None, None])
        s = jnp.einsum('bgrqd,bgqnld->bgrqnl', q_c, k_s).reshape(B, G, R, Qc, n_top * Ls)
        p = masked_softmax(s, mask.reshape(B, G, 1, Qc, n_top * Ls))
        return jnp.einsum('bgrqnl,bgqnld->bgrqd', p.reshape(B, G, R, Qc, n_top, Ls), v_s)

    o_slc = lax.map(sel_chunk, (jnp.arange(nq),
                                jnp.moveaxis(qg.reshape(B, G, R, nq, Qc, Dh), 3, 0),
                                jnp.moveaxis(sel_idx.reshape(B, G, nq, Qc, n_top), 2, 0),
                                jnp.moveaxis(sel_ok.reshape(B, G, nq, Qc, n_top), 2, 0)))
    o_slc = jnp.moveaxis(o_slc, 0, 3).reshape(B, G, R, S, Dh)

    Qw = WIN_Q_BLOCK
    nw = S // Qw
    kw_p = jnp.pad(kv(k_win), ((0, 0), (0, 0), (W, 0), (0, 0)))
    vw_p = jnp.pad(kv(v_win), ((0, 0), (0, 0), (W, 0), (0, 0)))

    def win_block(args):
        c, q_c = args
        s0 = c * Qw
        k_w = lax.dynamic_slice_in_dim(kw_p, s0, Qw + W, axis=2)
        v_w = lax.dynamic_slice_in_dim(vw_p, s0, Qw + W, axis=2)
        t_c = s0 + jnp.arange(Qw)
        s_pos = s0 - W + jnp.arange(Qw + W)
        d = t_c[:, None] - s_pos[None, :]
        mask = (d >= 0) & (d < W) & (s_pos[None, :] >= 0)
        p = masked_softmax(jnp.einsum('bgrqd,bgkd->bgrqk', q_c, k_w), mask)
        return jnp.einsum('bgrqk,bgkd->bgrqd', p, v_w)

    o_win = lax.map(win_block, (jnp.arange(nw), jnp.moveaxis(qg.reshape(B, G, R, nw, Qw, Dh), 3, 0)))
    o_win = jnp.moveaxis(o_win, 0, 3).reshape(B, G, R, S, Dh)

    g = jnp.moveaxis(gates, 1, 2).reshape(B, G, R, S, 3)
    o = g[..., 0:1] * o_cmp + g[..., 1:2] * o_slc + g[..., 2:3] * o_win
    return o.reshape(B, H, S, Dh).transpose(0, 2, 1, 3).reshape(B, S, H * Dh)


def even_mixer(h, w_in, shift_mu, w0, w_up, a0, a_up, g_up, k_k, k_a, r_k, ln_g, ln_b, w_out):
    B, S, _ = h.shape
    qa, ka, va, zb = split_cols(h @ w_in, EVEN_SPLITS)
    heads = lambda u: u.reshape(B, S, -1, HEAD_DIM)
    o_a = moba_attention(heads(qa), heads(ka), heads(va))
    zb = zb + (token_shift(zb) - zb) * shift_mu
    r, kb, vb, w_lo, a_lo, g_lo = split_cols(zb, RWKV_SPLITS)
    o_b = rwkv7_time_mix(r, kb, vb, w_lo, a_lo, g_lo, w0, w_up, a0, a_up, g_up, k_k, k_a, r_k, ln_g, ln_b)
    return (jnp.concatenate([o_a, o_b], axis=-1) @ w_out).astype(h.dtype)


def odd_mixer(h, w_in, pe_k, w1_k, w2_k, pe_v, w1_v, w2_v, w_out):
    B, S, _ = h.shape
    rq, rk, rv, rg, nq, kc, vc, ks, vs, kw, vw, ng = split_cols(h @ w_in, ODD_SPLITS)
    o_c = retention(rq.reshape(B, S, RET_HEADS, RET_QK_DIM), rk.reshape(B, S, RET_HEADS, RET_QK_DIM),
                    rv.reshape(B, S, RET_HEADS, RET_V_DIM))
    o_c = jax.nn.silu(rg) * o_c.reshape(B, S, RET_W)
    hd = lambda u: u.reshape(B, S, -1, HEAD_DIM)
    gates = jax.nn.sigmoid(ng.reshape(B, S, NSA_HEADS, 3))
    o_d = nsa_attention(hd(nq), hd(kc), hd(vc), hd(ks), hd(vs), hd(kw), hd(vw), gates,
                        pe_k, w1_k, w2_k, pe_v, w1_v, w2_v)
    return (jnp.concatenate([o_c, o_d], axis=-1) @ w_out).astype(h.dtype)


def setup_inputs(seed: int = 0) -> dict:
    key = jax.random.key(seed)
    keys = list(jax.random.split(key, 32))

    def nrm(shape, scale):
        return jax.random.normal(keys.pop(), shape, jnp.float32) * scale

    def uni(shape, lo, hi):
        return jax.random.uniform(keys.pop(), shape, jnp.float32, lo, hi)

    E, O, D = N_EVEN, N_ODD, D_MODEL
    cmp_in = NSA_CMP_BLOCK * HEAD_DIM
    return {
        'x': nrm((BATCH, SEQ, D), 1.0),
        'mix_norm': 1.0 + nrm((DEPTH, D), 0.02),
        'ffn_norm': 1.0 + nrm((DEPTH, D), 0.02),
        'even_w_in': nrm((E, D, EVEN_IN), D ** -0.5),
        'even_shift_mu': uni((E, RWKV_SHIFT_W), 0.0, 1.0),
        'even_w0': uni((E, RWKV_W), -6.0, -1.0),
        'even_w_up': nrm((E, RWKV_DECAY_LORA, RWKV_W), 0.1 * RWKV_DECAY_LORA ** -0.5),
        'even_a0': nrm((E, RWKV_W), 0.1),
        'even_a_up': nrm((E, RWKV_ICLR_LORA, RWKV_W), 0.1 * RWKV_ICLR_LORA ** -0.5),
        'even_g_up': nrm((E, RWKV_GATE_LORA, RWKV_W), RWKV_GATE_LORA ** -0.5),
        'even_k_k': 0.85 + nrm((E, RWKV_W), 0.05),
        'even_k_a': 1.0 + nrm((E, RWKV_W), 0.05),
        'even_r_k': nrm((E, RWKV_HEADS, HEAD_DIM), 0.1),
        'even_ln_g': 1.0 + nrm((E, RWKV_W), 0.02),
        'even_ln_b': nrm((E, RWKV_W), 0.02),
        'even_w_out': nrm((E, MOBA_W + RWKV_W, D), (MOBA_W + RWKV_W) ** -0.5),
        'odd_w_in': nrm((O, D, ODD_IN), D ** -0.5),
        'odd_cmp_pe_k': nrm((O, NSA_CMP_BLOCK, HEAD_DIM), 0.02),
        'odd_cmp_w1_k': nrm((O, cmp_in, NSA_CMP_HIDDEN), cmp_in ** -0.5),
        'odd_cmp_w2_k': nrm((O, NSA_CMP_HIDDEN, HEAD_DIM), NSA_CMP_HIDDEN ** -0.5),
        'odd_cmp_pe_v': nrm((O, NSA_CMP_BLOCK, HEAD_DIM), 0.02),
        'odd_cmp_w1_v': nrm((O, cmp_in, NSA_CMP_HIDDEN), cmp_in ** -0.5),
        'odd_cmp_w2_v': nrm((O, NSA_CMP_HIDDEN, HEAD_DIM), NSA_CMP_HIDDEN ** -0.5),
        'odd_w_out': nrm((O, RET_W + NSA_W, D), (RET_W + NSA_W) ** -0.5),
        'ffn_w1': nrm((DEPTH, D, FFN_HIDDEN), D ** -0.5),
        'ffn_w3': nrm((DEPTH, D, FFN_HIDDEN), D ** -0.5),
        'ffn_w2': nrm((DEPTH, FFN_HIDDEN, D), FFN_HIDDEN ** -0.5),
        'final_norm': 1.0 + nrm((D,), 0.02),
    }


def reference(x, mix_norm, ffn_norm, even_w_in, even_shift_mu, even_w0, even_w_up, even_a0, even_a_up,
              even_g_up, even_k_k, even_k_a, even_r_k, even_ln_g, even_ln_b, even_w_out, odd_w_in,
              odd_cmp_pe_k, odd_cmp_w1_k, odd_cmp_w2_k, odd_cmp_pe_v, odd_cmp_w1_v, odd_cmp_w2_v, odd_w_out,
              ffn_w1, ffn_w3, ffn_w2, final_norm):
    for layer in range(DEPTH):
        i = layer // 2
        h = rms_norm(x, mix_norm[layer])
        if layer % 2 == 0:
            mix = even_mixer(h, even_w_in[i], even_shift_mu[i], even_w0[i], even_w_up[i], even_a0[i],
                             even_a_up[i], even_g_up[i], even_k_k[i], even_k_a[i], even_r_k[i],
                             even_ln_g[i], even_ln_b[i], even_w_out[i])
        else:
            mix = odd_mixer(h, odd_w_in[i], odd_cmp_pe_k[i], odd_cmp_w1_k[i], odd_cmp_w2_k[i],
                            odd_cmp_pe_v[i], odd_cmp_w1_v[i], odd_cmp_w2_v[i], odd_w_out[i])
        x = x + mix
        x = x + swiglu(rms_norm(x, ffn_norm[layer]), ffn_w1[layer], ffn_w3[layer], ffn_w2[layer]).astype(x.dtype)
    return rms_norm(x, final_norm)
```

```python
import contextlib
import numpy as np
import concourse.bass as bass
import concourse.mybir as mybir
from concourse.bass_utils import run_bass_kernel_spmd

F32 = mybir.dt.float32
BF16 = mybir.dt.bfloat16
ALU = mybir.AluOpType
AF = mybir.ActivationFunctionType
AX = mybir.AxisListType

D = 2048
S = 2048
NB = 2
FF = 5632
NFC = FF // 128
NC = D // 128
TB = 512
NTB = S // TB
EPS = 1e-6

SEM_LIMIT = 30000
NDMA_SLOTS = 12


class Buf:
    def __init__(self, t, name):
        self.t = t
        self.name = name
        self.w = {}
        self.r = {}
        self.psum = False
        self.is_dram = False
        self.last_eng = None

    def __getitem__(self, idx):
        return self.t[idx]


class _Sub:
    def __init__(self, b, g):
        self.__dict__["b"] = b
        self.__dict__["g"] = g

    def __getattr__(self, k):
        return getattr(self.b, k)

    def __setattr__(self, k, v):
        setattr(self.b, k, v)

    def __getitem__(self, idx):
        p, c = idx
        return self.b.t[p, self.g, c]


class Eng:
    def __init__(self, K, name, h):
        self.name = name
        self.h = h
        self.sem = K.new_sem(name)
        self.count = 0
        self.seen = {}
        self.slots = None
        self.dma_i = 0


class KB:
    def __init__(self, nc):
        self.nc = nc
        self.es = contextlib.ExitStack()
        self.stacks = [self.es]
        self.nsem = 0
        self.engs = {}
        for name, h in (("pe", nc.tensor), ("act", nc.scalar), ("dve", nc.vector),
                        ("pool", nc.gpsimd), ("sp", nc.sync)):
            self.engs[name] = Eng(self, name, h)
        self.nbuf = 0
        self.ninst = 0
        self.split_stores = True

    def new_sem(self, name):
        self.nsem += 1
        return self.es.enter_context(self.nc.semaphore(f"s{self.nsem}_{name}"))

    def sb(self, shape, dt, name=None):
        self.nbuf += 1
        name = f"{name or 'sb'}_{self.nbuf}"
        return Buf(self.stacks[-1].enter_context(self.nc.sbuf_tensor(name, list(shape), dt)), name)

    def ps(self, shape, dt=F32, name=None):
        self.nbuf += 1
        name = f"{name or 'ps'}_{self.nbuf}"
        b = Buf(self.stacks[-1].enter_context(self.nc.psum_tensor(name, list(shape), dt)), name)
        b.psum = True
        return b

    def dram(self, shape, dt, name, kind="Internal"):
        b = Buf(self.nc.dram_tensor(name, list(shape), dt, kind=kind), name)
        b.is_dram = True
        return b

    @contextlib.contextmanager
    def scope(self):
        st = contextlib.ExitStack()
        self.stacks.append(st)
        try:
            yield
        finally:
            self.barrier()
            self.stacks.pop()
            st.close()

    def _wait(self, e, deps):
        for key, (sem, v) in deps.items():
            if e.seen.get(key, 0) >= v:
                continue
            if e.name == "pe" and sem is e.sem:
                continue
            e.h.wait_ge(sem, v)
            e.seen[key] = v

    @staticmethod
    def _merge(deps, d):
        for key, (sem, v) in d.items():
            if key not in deps or deps[key][1] < v:
                deps[key] = (sem, v)

    def _deps(self, rd, wr):
        deps = {}
        for b in rd:
            self._merge(deps, b.w)
            if b.psum:
                self._merge(deps, b.r)
        for b in wr:
            self._merge(deps, b.w)
            self._merge(deps, b.r)
        return deps

    def _commit(self, tok, rd, wr):
        key = id(tok[0])
        for b in rd:
            b.r[key] = tok
        for b in wr:
            b.w = {key: tok}
            b.r = {}

    def op(self, eng, fn, rd=(), wr=()):
        e = self.engs[eng]
        self._wait(e, self._deps(rd, wr))
        if e.count >= SEM_LIMIT:
            e.sem = self.new_sem(e.name)
            e.count = 0
        inst = fn(e.h)
        e.count += 1
        inst.then_inc(e.sem, 1)
        self.ninst += 1
        self._commit((e.sem, e.count), rd, wr)
        for b in wr:
            b.last_eng = eng

    def dma(self, q, out, in_, rd=(), wr=(), **kw):
        if q == "sp" and self.split_stores and wr and wr[0].is_dram and rd and not rd[0].is_dram:
            q = "act" if rd[0].last_eng == "act" else "pool"
        e = self.engs[q]
        if e.slots is None:
            e.slots = [[self.new_sem(f"dma_{q}{i}"), 0] for i in range(NDMA_SLOTS)]
        slot = e.slots[e.dma_i % NDMA_SLOTS]
        e.dma_i += 1
        deps = self._deps(rd, wr)
        if slot[1] > 0:
            self._merge(deps, {id(slot[0]): (slot[0], slot[1])})
        self._wait(e, deps)
        if slot[1] + 16 > SEM_LIMIT:
            slot[0] = self.new_sem(f"dma_{q}")
            slot[1] = 0
        inst = e.h.dma_start(out=out, in_=in_, **kw)
        slot[1] += 16
        inst.then_inc(slot[0], 16)
        self.ninst += 1
        self._commit((slot[0], slot[1]), rd, wr)

    def barrier(self):
        deps = {}
        for e in self.engs.values():
            if e.count > 0:
                deps[id(e.sem)] = (e.sem, e.count)
            if e.slots:
                for s in e.slots:
                    if s[1] > 0:
                        deps[id(s[0])] = (s[0], s[1])
        for e in self.engs.values():
            d = {k: v for k, v in deps.items() if not (v[0] is e.sem)}
            self._wait(e, d)

    def close(self):
        self.es.close()


def tile_lhsT(w):
    Kd, N = w.shape
    return np.ascontiguousarray(w.reshape(Kd // 128, 128, N // 128, 128).transpose(2, 1, 0, 3))


def col_vec(v):
    return np.ascontiguousarray(v.reshape(-1, 128).T)


class Net:
    def __init__(self, K, cfg):
        self.K = K
        self.cfg = cfg
        nc = K.nc
        self.inp = {}
        self.pending_casts = []

    def ext(self, name, shape, dt=F32):
        b = self.K.dram(shape, dt, name, kind="ExternalInput")
        self.inp[name] = b
        return b

    def cast_dram(self, src, dst, shape, defer=False):
        K = self.K
        n = int(np.prod(shape))
        assert n % 2048 == 0
        rows = n // 2048
        names = " ".join(f"a{i}" for i in range(len(shape)))
        sv = src.t.ap().rearrange(f"{names} -> ({names})").rearrange("(r c) -> r c", c=2048)
        dv = dst.t.ap().rearrange(f"{names} -> ({names})").rearrange("(r c) -> r c", c=2048)
        R = 1024
        for r0 in range(0, rows, R):
            r1 = min(rows, r0 + R)
            thunk = (lambda r0=r0, r1=r1: K.dma("pool", dv[r0:r1, :], sv[r0:r1, :], rd=[src], wr=[dst]))
            if defer:
                self.pending_casts.append(thunk)
            else:
                thunk()

    def drip_cast(self, n=1):
        for _ in range(n):
            if self.pending_casts:
                self.pending_casts.pop(0)()

    def consts(self):
        K = self.K
        self.ones_bf = K.sb([128, 128], BF16, "ones")
        K.op("pool", lambda e: e.memset(self.ones_bf[:, :], 1.0), wr=[self.ones_bf])
        self.ident = K.sb([128, 128], F32, "ident")
        K.op("pool", lambda e: e.memset(self.ident[:, :], 0.0), wr=[self.ident])
        K.op("pool", lambda e: e.affine_select(out=self.ident[:, :], in_=self.ident[:, :], pattern=[[-1, 128]],
                                                 compare_op=ALU.not_equal, fill=1.0, base=0, channel_multiplier=1),
             rd=[self.ident], wr=[self.ident])
        self.pb = [K.ps([128, 512], F32, f"bank{i}") for i in range(8)]

    def prologue_x(self, x, xT):
        K = self.K
        with K.scope():
            xin = [K.sb([128, D], F32, "xin") for _ in range(2)]
            st = [K.sb([128, NC, TB], F32, "xst") for _ in range(2)]
            i = 0
            for b in range(NB):
                for tb in range(NTB):
                    stg = st[(b * NTB + tb) % 2]
                    for tt in range(TB // 128):
                        t0 = tb * TB + tt * 128
                        xi = xin[i % 2]
                        i += 1
                        K.dma("sp", xi[:, :], x[b, t0:t0 + 128, :], rd=[x], wr=[xi])
                        for cg in range(4):
                            bank = self.pb[cg % 4 + 4 * (i % 2)]
                            for j in range(4):
                                c = cg * 4 + j
                                K.op("pe", lambda e, c=c, j=j, bank=bank, xi=xi: e.transpose(
                                    out=bank[:, j * 128:(j + 1) * 128], in_=xi[:, c * 128:(c + 1) * 128],
                                    identity=self.ident[:, :]), rd=[xi, self.ident], wr=[bank])
                            eng = "act" if cg % 2 == 0 else "dve"
                            src = bank[:, :].rearrange("p (j t) -> p j t", j=4)
                            dst = stg[:, cg * 4:(cg + 1) * 4, tt * 128:(tt + 1) * 128]
                            if eng == "act":
                                K.op("act", lambda e, s=src, d=dst: e.activation(out=d, in_=s, func=AF.Copy),
                                     rd=[bank], wr=[stg])
                            else:
                                K.op("dve", lambda e, s=src, d=dst: e.tensor_copy(out=d, in_=s), rd=[bank], wr=[stg])
                    K.dma("sp", xT[b].t.ap().rearrange("c p t -> p c t")[:, :, tb * TB:(tb + 1) * TB], stg[:, :, :],
                          rd=[stg], wr=[xT[b]])

    def norm_block(self, xTb, tb, gbuf, goff, xblk, sq, rstd, emit):
        K = self.K
        K.dma("sp", xblk[:, :, :], xTb.t.ap().rearrange("c p t -> p c t")[:, :, tb * TB:(tb + 1) * TB],
              rd=[xTb], wr=[xblk])
        bank = self.pb[6]
        for c in range(NC):
            s = sq[c % 2]
            K.op("act", lambda e, c=c, s=s: e.activation(out=s[:, :], in_=xblk[:, c, :], func=AF.Square),
                 rd=[xblk], wr=[s])
            K.op("pe", lambda e, c=c, s=s: e.matmul(bank[:, :], lhsT=self.ones_bf[:, :], rhs=s[:, :],
                                                    start=(c == 0), stop=(c == NC - 1)),
                 rd=[s, self.ones_bf], wr=[bank])
        K.op("act", lambda e: e.activation(out=rstd[:, :], in_=bank[:, :], func=AF.Sqrt, bias=EPS, scale=1.0 / D),
             rd=[bank], wr=[rstd])
        K.op("dve", lambda e: e.reciprocal(out=rstd[:, :], in_=rstd[:, :]), rd=[rstd], wr=[rstd])
        for c in range(NC):
            emit(c, xblk[:, c, :], gbuf[:, goff + c:goff + c + 1], rstd[:, :])

    def ffn(self, layer, xT, w1t, w3t, w2t, gbuf, goff):
        K = self.K
        with K.scope():
            xblks = [K.sb([128, NC, TB], F32, "xblk") for _ in range(2)]
            hTs = [K.sb([128, NC, TB], BF16, "hT") for _ in range(2)]
            rstds = [K.sb([128, TB], F32, "rstd") for _ in range(2)]
            sq = [K.sb([128, TB], BF16, "sq") for _ in range(2)]
            actT = K.sb([128, NFC, TB], BF16, "actT")
            w1b = [K.sb([128, NC, 128], BF16, "w1b") for _ in range(2)]
            w3b = [K.sb([128, NC, 128], BF16, "w3b") for _ in range(2)]
            w2b = [K.sb([128, NFC, 128], BF16, "w2b") for _ in range(2)]
            sil = [K.sb([128, TB], F32, "sil") for _ in range(2)]
            blocks = [(b, tb) for b in range(NB) for tb in range(NTB)]

            def norm(i):
                b, tb = blocks[i]
                xblk, hT, rstd = xblks[i % 2], hTs[i % 2], rstds[i % 2]

                def emit(c, x_ap, g_ap, r_ap):
                    K.op("dve", lambda e: e.scalar_tensor_tensor(out=hT[:, c, :], in0=x_ap, scalar=g_ap, in1=r_ap,
                                                                op0=ALU.mult, op1=ALU.mult), rd=[xblk, rstd, gbuf], wr=[hT])
                self.norm_block(xT[b], tb, gbuf, goff, xblk, sq, rstd, emit)

            norm(0)
            for i, (b, tb) in enumerate(blocks):
                xblk, hT = xblks[i % 2], hTs[i % 2]
                for fc in range(NFC):
                    wa, wb_ = w1b[fc % 2], w3b[fc % 2]
                    K.dma("sp", wa[:, :, :], w1t[layer, fc], rd=[w1t], wr=[wa])
                    K.dma("sp", wb_[:, :, :], w3t[layer, fc], rd=[w3t], wr=[wb_])
                    pu, pv = self.pb[fc % 2], self.pb[2 + fc % 2]
                    for c in range(NC):
                        K.op("pe", lambda e: e.matmul(pu[:, :], lhsT=wa[:, c, :], rhs=hT[:, c, :], start=(c == 0), stop=(c == NC - 1)),
                             rd=[wa, hT], wr=[pu])
                    for c in range(NC):
                        K.op("pe", lambda e: e.matmul(pv[:, :], lhsT=wb_[:, c, :], rhs=hT[:, c, :], start=(c == 0), stop=(c == NC - 1)),
                             rd=[wb_, hT], wr=[pv])
                    sl = sil[fc % 2]
                    K.op("act", lambda e: e.activation(out=sl[:, :], in_=pu[:, :], func=AF.Silu), rd=[pu], wr=[sl])
                    K.op("dve", lambda e: e.tensor_tensor(out=actT[:, fc, :], in0=sl[:, :], in1=pv[:, :], op=ALU.mult),
                         rd=[pv, sl], wr=[actT])
                if i + 1 < len(blocks):
                    norm(i + 1)
                for fo in range(NC):
                    w2 = w2b[fo % 2]
                    K.dma("sp", w2[:, :, :], w2t[layer, fo], rd=[w2t], wr=[w2])
                    py = self.pb[4 + fo % 2]
                    for fc in range(NFC):
                        K.op("pe", lambda e: e.matmul(py[:, :], lhsT=w2[:, fc, :], rhs=actT[:, fc, :], start=(fc == 0), stop=(fc == NFC - 1)),
                             rd=[w2, actT], wr=[py])
                    K.op("dve", lambda e: e.tensor_tensor(out=xblk[:, fo, :], in0=xblk[:, fo, :], in1=py[:, :], op=ALU.add),
                         rd=[py, xblk], wr=[xblk])
                K.dma("sp", xT[b].t.ap().rearrange("c p t -> p c t")[:, :, tb * TB:(tb + 1) * TB], xblk[:, :, :],
                      rd=[xblk], wr=[xT[b]])

    def final(self, xT, gbuf, goff, out):
        K = self.K
        with K.scope():
            xblk = K.sb([128, NC, TB], F32, "xblk")
            sq = [K.sb([128, TB], BF16, "sq") for _ in range(2)]
            rstd = K.sb([128, TB], F32, "rstd")
            yb = K.sb([128, NC, TB], F32, "yb")
            ot = [K.sb([128, D], F32, "ot") for _ in range(2)]
            i = 0
            for b in range(NB):
                for tb in range(NTB):
                    def emit(c, x_ap, g_ap, r_ap):
                        K.op("dve", lambda e: e.scalar_tensor_tensor(out=yb[:, c, :], in0=x_ap, scalar=g_ap, in1=r_ap,
                                                                    op0=ALU.mult, op1=ALU.mult),
                             rd=[xblk, rstd, gbuf], wr=[yb])
                    self.norm_block(xT[b], tb, gbuf, goff, xblk, sq, rstd, emit)
                    for tt in range(TB // 128):
                        o = ot[i % 2]
                        i += 1
                        for cg in range(4):
                            bank = self.pb[cg % 4 + 4 * (i % 2)] if False else self.pb[cg]
                            for j in range(4):
                                c = cg * 4 + j
                                K.op("pe", lambda e, c=c, j=j, bank=bank: e.transpose(
                                    out=bank[:, j * 128:(j + 1) * 128], in_=yb[:, c, tt * 128:(tt + 1) * 128],
                                    identity=self.ident[:, :]), rd=[yb, self.ident], wr=[bank])
                            if cg % 2 == 0:
                                K.op("act", lambda e, bank=bank, o=o, cg=cg: e.activation(
                                    out=o[:, cg * 512:(cg + 1) * 512], in_=bank[:, :], func=AF.Copy), rd=[bank], wr=[o])
                            else:
                                K.op("dve", lambda e, bank=bank, o=o, cg=cg: e.tensor_copy(
                                    out=o[:, cg * 512:(cg + 1) * 512], in_=bank[:, :]), rd=[bank], wr=[o])
                        t0 = tb * TB + tt * 128
                        K.dma("sp", out[b, t0:t0 + 128, :], o[:, :], rd=[o], wr=[out])


    @staticmethod
    def pipeline(items):
        for i, (a, _) in enumerate(items):
            a()
            if i > 0:
                items[i - 1][1]()
        if items:
            items[-1][1]()

    def mm_fm(self, wsrc, w_ap, wbufs, hT, i, nk=NC):
        K = self.K
        wb = wbufs[i % len(wbufs)]
        K.dma("sp", wb[:, 0:nk, :], w_ap, rd=[wsrc], wr=[wb])
        bank = self.pb[i % 4]
        for c in range(nk):
            K.op("pe", lambda e: e.matmul(bank[:, :], lhsT=wb[:, c, :], rhs=hT[:, c, :], start=(c == 0),
                                          stop=(c == nk - 1)), rd=[wb, hT], wr=[bank])
        return bank

    def norm_bufs(self):
        K = self.K
        return (K.sb([128, NC, TB], F32, "xblk"), [K.sb([128, TB], BF16, "sq") for _ in range(2)],
                K.sb([128, TB], F32, "rstd"), K.sb([128, NC, TB], BF16, "hT"))

    def norm_bufs2(self):
        K = self.K
        a = self.norm_bufs()
        return [a, (a[0], a[1], K.sb([128, TB], F32, "rstd"), K.sb([128, NC, TB], BF16, "hT"))]

    def norm_to_hT(self, xTb, tb, gbuf, goff, nb):
        K = self.K
        xblk, sq, rstd, hT = nb

        def emit(c, x_ap, g_ap, r_ap):
            K.op("dve", lambda e: e.scalar_tensor_tensor(out=hT[:, c, :], in0=x_ap, scalar=g_ap, in1=r_ap,
                                                        op0=ALU.mult, op1=ALU.mult), rd=[xblk, rstd, gbuf], wr=[hT])
        self.norm_block(xTb, tb, gbuf, goff, xblk, sq, rstd, emit)

    def outproj(self, xT, mix, wout, nchunks=NC, c0=0):
        K = self.K
        with K.scope():
            xblks = [K.sb([128, NC, TB], F32, "xblk") for _ in range(2)]
            mTs = [K.sb([128, NC, TB], BF16, "mT") for _ in range(2)]
            mt = [K.sb([128, D], F32, "mt") for _ in range(2)]
            wbufs = [K.sb([128, NC, 128], BF16, "wo") for _ in range(3)]
            blocks = [(b, tb) for b in range(NB) for tb in range(NTB)]
            xview = lambda b, tb: xT[b].t.ap().rearrange("c p t -> p c t")[:, :, tb * TB:(tb + 1) * TB]

            def prep(i):
                b, tb = blocks[i]
                xblk, mT = xblks[i % 2], mTs[i % 2]
                for tt in range(4):
                    m = mt[tt % 2]
                    t0 = tb * TB + tt * 128
                    K.dma("sp", m[:, c0 * 128:(c0 + nchunks) * 128], mix[b][t0:t0 + 128, c0 * 128:(c0 + nchunks) * 128], rd=[mix[b]], wr=[m])
                    for cg in range(nchunks // 4):
                        bank = self.pb[4 + cg % 4]
                        for j in range(4):
                            c = cg * 4 + j
                            K.op("pe", lambda e: e.transpose(out=bank[:, j * 128:(j + 1) * 128],
                                                             in_=m[:, (c0 + c) * 128:(c0 + c + 1) * 128], identity=self.ident[:, :]),
                                 rd=[m, self.ident], wr=[bank])
                        src = bank[:, :].rearrange("p (j t) -> p j t", j=4)
                        dst = mT[:, cg * 4:(cg + 1) * 4, tt * 128:(tt + 1) * 128]
                        if cg % 2 == 0:
                            K.op("act", lambda e: e.activation(out=dst, in_=src, func=AF.Copy), rd=[bank], wr=[mT])
                        else:
                            K.op("dve", lambda e: e.tensor_copy(out=dst, in_=src), rd=[bank], wr=[mT])
                K.dma("sp", xblk[:, :, :], xview(b, tb), rd=[xT[b]], wr=[xblk])

            prep(0)
            wi = 0
            for i, (b, tb) in enumerate(blocks):
                xblk, mT = xblks[i % 2], mTs[i % 2]
                for oc in range(NC):
                    if oc == NC // 2 and i + 1 < len(blocks):
                        prep(i + 1)
                    bank = self.mm_fm(wout, wout[oc, :, c0:c0 + nchunks, :], wbufs, mT, wi, nk=nchunks)
                    wi += 1
                    K.op("dve", lambda e: e.tensor_tensor(out=xblk[:, oc, :], in0=xblk[:, oc, :], in1=bank[:, :],
                                                          op=ALU.add), rd=[bank, xblk], wr=[xblk])
                K.dma("sp", xview(b, tb), xblk[:, :, :], rd=[xblk], wr=[xT[b]])

    def inproj_even(self, xT, gbuf, goff, W, SC):
        K = self.K
        with K.scope():
            nbs = self.norm_bufs2()
            blocks = [(b, tb) for b in range(NB) for tb in range(NTB)]
            wbufs = [K.sb([128, NC, 128], BF16, "wi") for _ in range(3)]
            wv = K.sb([128, NC, 1024], BF16, "wv")
            K.dma("sp", wv[:, :, :], W["v"][:, :, :], rd=[W["v"]], wr=[wv])
            mu = K.sb([128, 26], F32, "mu")
            K.dma("sp", mu[:, :], W["mu"][:, :], rd=[W["mu"]], wr=[mu])
            st16 = [K.sb([128, TB], BF16, "st16") for _ in range(2)]
            zs = [K.sb([128, TB + 1], F32, "zs") for _ in range(2)]
            zst = [K.sb([128, TB], F32, "zst") for _ in range(2)]
            vst = [K.sb([128, 1024], BF16, "vst") for _ in range(2)]
            zsave = K.sb([128, 26], F32, "zsave")
            i = 0
            self.norm_to_hT(xT[0], 0, gbuf, goff, nbs[0])
            for b in range(NB):
                K.op("pool", lambda e: e.memset(zsave[:, :], 0.0), wr=[zsave])
                for tb in range(NTB):
                    tsl = slice(tb * TB, (tb + 1) * TB)
                    bi_ = b * NTB + tb
                    hT = nbs[bi_ % 2][3]
                    for oc in range(16):
                        bank = self.mm_fm(W["qkz"], W["qkz"][oc], wbufs, hT, i)
                        i += 1
                        stg = st16[oc % 2]
                        K.op("act", lambda e: e.activation(out=stg[:, :], in_=bank[:, :], func=AF.Copy,
                                                           scale=(0.125 if oc < 8 else 1.0)), rd=[bank], wr=[stg])
                        dst = SC["q0"][b] if oc < 8 else SC["k0"][b]
                        K.dma("sp", dst[oc % 8, :, tsl], stg[:, :], rd=[stg], wr=[dst])
                    if bi_ + 1 < len(blocks):
                        self.norm_to_hT(xT[blocks[bi_ + 1][0]], blocks[bi_ + 1][1], gbuf, goff, nbs[(bi_ + 1) % 2])
                    for tt in range(4):
                        vs = vst[tt % 2]
                        for g2 in range(2):
                            bank = self.pb[4 + g2]
                            for c in range(NC):
                                K.op("pe", lambda e: e.matmul(bank[:, :], lhsT=hT[:, c, tt * 128:(tt + 1) * 128],
                                                              rhs=wv[:, c, g2 * 512:(g2 + 1) * 512], start=(c == 0),
                                                              stop=(c == NC - 1)), rd=[hT, wv], wr=[bank])
                            K.op("dve", lambda e: e.tensor_copy(out=vs[:, g2 * 512:(g2 + 1) * 512], in_=bank[:, :]),
                                 rd=[bank], wr=[vs])
                        t0 = tb * TB + tt * 128
                        K.dma("sp", SC["v0"][b][t0:t0 + 128, :], vs[:, :], rd=[vs], wr=[SC["v0"][b]])
                    for j in range(26):
                        bank = self.mm_fm(W["qkz"], W["qkz"][16 + j], wbufs, hT, i)
                        i += 1
                        z, zo = zs[j % 2], zst[j % 2]
                        K.op("act", lambda e: e.activation(out=z[:, 0:1], in_=zsave[:, j:j + 1], func=AF.Copy), rd=[zsave], wr=[z])
                        K.op("act", lambda e: e.activation(out=z[:, 1:TB + 1], in_=bank[:, :], func=AF.Copy),
                             rd=[bank], wr=[z])
                        K.op("act", lambda e: e.activation(out=zsave[:, j:j + 1], in_=z[:, TB:TB + 1], func=AF.Copy), rd=[z], wr=[zsave])
                        K.op("dve", lambda e: e.tensor_tensor(out=zo[:, :], in0=z[:, 0:TB], in1=z[:, 1:TB + 1],
                                                              op=ALU.subtract), rd=[z], wr=[zo])
                        K.op("dve", lambda e: e.scalar_tensor_tensor(out=zo[:, :], in0=zo[:, :], scalar=mu[:, j:j + 1],
                                                                    in1=z[:, 1:TB + 1], op0=ALU.mult, op1=ALU.add),
                             rd=[z, zo, mu], wr=[zo])
                        K.dma("sp", SC["z0"][b][j, :, tsl], zo[:, :], rd=[zo], wr=[SC["z0"][b]])

    def moba_consts(self):
        K = self.K
        self.identb = K.sb([128, 128], BF16, "identb")
        K.op("dve", lambda e: e.tensor_copy(out=self.identb[:, :], in_=self.ident[:, :]), rd=[self.ident], wr=[self.identb])
        self.tri = K.sb([128, 128], BF16, "tri")
        K.op("pool", lambda e: e.memset(self.tri[:, :], 1.0), wr=[self.tri])
        K.op("pool", lambda e: e.affine_select(out=self.tri[:, :], in_=self.tri[:, :], pattern=[[1, 128]],
                                                 compare_op=ALU.is_ge, fill=0.0, base=0, channel_multiplier=-1),
             rd=[self.tri], wr=[self.tri])
        self.pastm = K.sb([128, 16, 8], F32, "pastm")
        self.ownm = K.sb([128, 16, 8], F32, "ownm")
        self.negm = K.sb([128, 16, 8], F32, "negm")
        K.op("pool", lambda e: e.memset(self.pastm[:, :, :], 0.0), wr=[self.pastm])
        K.op("pool", lambda e: e.memset(self.ownm[:, :, :], 0.0), wr=[self.ownm])
        for t in range(16):
            n = t // 2
            if n > 0:
                K.op("pool", lambda e: e.memset(self.pastm[:, t, 0:n], 1.0), wr=[self.pastm])
            K.op("pool", lambda e: e.memset(self.ownm[:, t, n:n + 1], 1.0), wr=[self.ownm])
        K.op("dve", lambda e: e.tensor_scalar(out=self.negm[:, :, :], in0=self.pastm[:, :, :], scalar1=-1.0, scalar2=1e30,
                                              op0=ALU.add, op1=ALU.mult), rd=[self.pastm], wr=[self.negm])

    def moba(self, SC, mix):
        K = self.K
        with K.scope():
            qc_ = K.sb([128, S], BF16, "qc")
            klo = K.sb([128, S], BF16, "klo")
            khi = K.sb([128, S], BF16, "khi")
            K.op("pool", lambda e: e.memset(klo[64:128, :], 0.0), wr=[klo])
            K.op("pool", lambda e: e.memset(khi[0:64, :], 0.0), wr=[khi])
            vaug = K.sb([128, 16, 2, 65], BF16, "vaug")
            K.op("pool", lambda e: e.memset(vaug[:, :, :, 64:65], 1.0), wr=[vaug])
            kmean = K.sb([128, 8], F32, "kmean")
            kmbd = K.sb([128, 16], BF16, "kmbd")
            G = K.sb([128, 16, 2, 8], F32, "G")
            cmp = K.sb([128, 32, 8, 8], F32, "cmp")
            rank = K.sb([128, 16, 2, 8], F32, "rank")
            biasT = K.sb([16, S], BF16, "biasT")
            pbuf = [K.sb([128, 16, TB], BF16, "pbuf") for _ in range(2)]
            ot = [K.sb([128, 4, 128], F32, "ot") for _ in range(2)]
            rec = K.sb([128, 4], F32, "rec")
            it = 0
            for b in range(NB):
                for c in range(8):
                    K.dma("sp", qc_[:, :], SC["q0"][b][c], rd=[SC["q0"][b]], wr=[qc_])
                    K.dma("sp", klo[0:64, :], SC["k0"][b][c, 0:64, :], rd=[SC["k0"][b]], wr=[klo])
                    K.dma("sp", khi[64:128, :], SC["k0"][b][c, 64:128, :], rd=[SC["k0"][b]], wr=[khi])
                    for h2 in range(2):
                        vsrc = SC["v0"][b].t.ap().rearrange("(kt p) f -> p kt f", p=128)[:, :, (2 * c + h2) * 64:(2 * c + h2 + 1) * 64]
                        K.dma("sp", vaug[:, :, h2, 0:64], vsrc, rd=[SC["v0"][b]], wr=[vaug])
                    K.op("dve", lambda e: e.tensor_reduce(out=kmean[0:64, :], in_=klo[0:64, :].rearrange("p (n l) -> p n l", l=256),
                                                          axis=AX.X, op=ALU.add), rd=[klo], wr=[kmean])
                    K.op("dve", lambda e: e.tensor_reduce(out=kmean[64:128, :], in_=khi[64:128, :].rearrange("p (n l) -> p n l", l=256),
                                                          axis=AX.X, op=ALU.add), rd=[khi], wr=[kmean])
                    K.op("pool", lambda e: e.memset(kmbd[:, :], 0.0), wr=[kmbd])
                    K.op("dve", lambda e: e.tensor_copy(out=kmbd[0:64, 0:8], in_=kmean[0:64, :]), rd=[kmean], wr=[kmbd])
                    K.op("dve", lambda e: e.tensor_copy(out=kmbd[64:128, 8:16], in_=kmean[64:128, :]), rd=[kmean], wr=[kmbd])
                    gb = self.pb[6]
                    for t in range(16):
                        K.op("pe", lambda e: e.matmul(gb[:, t * 16:(t + 1) * 16], lhsT=qc_[:, t * 128:(t + 1) * 128],
                                                      rhs=kmbd[:, :], start=True, stop=True), rd=[qc_, kmbd], wr=[gb])
                    pm = self.pastm[:, :, :].unsqueeze(2).to_broadcast([128, 16, 2, 8])
                    K.op("dve", lambda e: e.tensor_tensor(out=G[:, :, :, :], in0=gb[:, 0:256].rearrange("p (t h n) -> p t h n", h=2, n=8),
                                                          in1=pm, op=ALU.mult), rd=[gb, self.pastm], wr=[G])
                    K.op("dve", lambda e: e.tensor_tensor(out=G[:, :, :, :], in0=G[:, :, :, :],
                                                          in1=self.negm[:, :, :].unsqueeze(2).to_broadcast([128, 16, 2, 8]),
                                                          op=ALU.add), rd=[G, self.negm], wr=[G])
                    g3 = G[:, :, :, :].rearrange("p t h n -> p (t h) n")
                    K.op("dve", lambda e: e.tensor_tensor(out=cmp[:, :, :, :], in0=g3.unsqueeze(2).to_broadcast([128, 32, 8, 8]),
                                                          in1=g3.unsqueeze(3).to_broadcast([128, 32, 8, 8]), op=ALU.is_gt),
                         rd=[G], wr=[cmp])
                    K.op("dve", lambda e: e.tensor_reduce(out=rank[:, :, :, :].rearrange("p t h n -> p (t h) n"), in_=cmp[:, :, :, :],
                                                          axis=AX.X, op=ALU.add), rd=[cmp], wr=[rank])
                    K.op("dve", lambda e: e.tensor_single_scalar(out=rank[:, :, :, :], in_=rank[:, :, :, :], scalar=2.5, op=ALU.is_lt),
                         rd=[rank], wr=[rank])
                    K.op("dve", lambda e: e.tensor_tensor(out=rank[:, :, :, :], in0=rank[:, :, :, :], in1=pm, op=ALU.mult),
                         rd=[rank, self.pastm], wr=[rank])
                    K.op("dve", lambda e: e.tensor_tensor(out=rank[:, :, :, :], in0=rank[:, :, :, :],
                                                          in1=self.ownm[:, :, :].unsqueeze(2).to_broadcast([128, 16, 2, 8]),
                                                          op=ALU.add), rd=[rank, self.ownm], wr=[rank])
                    K.op("dve", lambda e: e.tensor_scalar(out=rank[:, :, :, :], in0=rank[:, :, :, :], scalar1=-1.0, scalar2=30000.0,
                                                          op0=ALU.add, op1=ALU.mult), rd=[rank], wr=[rank])
                    for tg in range(4):
                        bank = self.pb[tg % 2]
                        for j in range(4):
                            t = tg * 4 + j
                            K.op("pe", lambda e: e.transpose(out=bank[0:16, j * 128:(j + 1) * 128],
                                                             in_=rank[:, t, :, :].rearrange("p h n -> p (h n)"),
                                                             identity=self.ident[:, :]), rd=[rank, self.ident], wr=[bank])
                        K.op("act", lambda e: e.activation(out=biasT[:, tg * 512:(tg + 1) * 512], in_=bank[0:16, :], func=AF.Copy),
                             rd=[bank], wr=[biasT])
                    items = []
                    for qc in range(4):
                        for h2 in range(2):
                            pb_ = pbuf[it % 2]
                            ob = self.pb[4 + (it % 2)]
                            it += 1

                            def stA(qc=qc, h2=h2, pb_=pb_):
                                kk = klo if h2 == 0 else khi
                                qs = slice(qc * TB, (qc + 1) * TB)
                                for kt in range(4 * qc + 4):
                                    bank = self.pb[kt % 2]
                                    jrow = h2 * 8 + kt // 2
                                    K.op("pe", lambda e: e.matmul(bank[:, :], lhsT=kk[:, kt * 128:(kt + 1) * 128], rhs=qc_[:, qs],
                                                                  start=True, stop=False), rd=[kk, qc_], wr=[bank])
                                    K.op("pe", lambda e: e.matmul(bank[:, :], lhsT=self.identb[0:16, jrow:jrow + 1].to_broadcast([16, 128]),
                                                                  rhs=biasT[:, qs], start=False, stop=True),
                                         rd=[self.identb, biasT], wr=[bank])
                                    K.op("act", lambda e: e.activation(out=pb_[:, kt, :], in_=bank[:, :], func=AF.Exp), rd=[bank], wr=[pb_])
                                    s_ = kt - 4 * qc
                                    if s_ >= 0:
                                        K.op("pool", lambda e: e.tensor_tensor(out=pb_[:, kt, s_ * 128:(s_ + 1) * 128],
                                                                               in0=pb_[:, kt, s_ * 128:(s_ + 1) * 128], in1=self.tri[:, :],
                                                                               op=ALU.mult), rd=[pb_, self.tri], wr=[pb_])

                            def stB(qc=qc, h2=h2, pb_=pb_, ob=ob, b=b, c=c):
                                o = ot[qc % 2]
                                obv = ob[:, 0:260].rearrange("p (s e) -> p s e", e=65)
                                for s_ in range(4):
                                    last = 4 * qc + s_
                                    for kt in range(last + 1):
                                        K.op("pe", lambda e: e.matmul(obv[:, s_, :], lhsT=pb_[:, kt, s_ * 128:(s_ + 1) * 128],
                                                                      rhs=vaug[:, kt, h2, :], start=(kt == 0), stop=(kt == last)),
                                             rd=[pb_, vaug], wr=[ob])
                                K.op("dve", lambda e: e.reciprocal(out=rec[:, :], in_=obv[:, :, 64]), rd=[ob], wr=[rec])
                                K.op("dve", lambda e: e.tensor_tensor(out=o[:, :, h2 * 64:(h2 + 1) * 64], in0=obv[:, :, 0:64],
                                                                      in1=rec[:, :].unsqueeze(2).to_broadcast([128, 4, 64]), op=ALU.mult),
                                     rd=[ob, rec], wr=[o])
                                if h2 == 1:
                                    dst = mix[b].t.ap().rearrange("(s p) f -> p s f", p=128)[:, qc * 4:(qc + 1) * 4, c * 128:(c + 1) * 128]
                                    K.dma("sp", dst, o[:, :, :], rd=[o], wr=[mix[b]])
                            items.append((stA, stB))
                    self.pipeline(items)


    def rwkv_chunk_setup(self):
        K = self.K
        CH = {}
        for nm, shp in (("XT", [32, 64, 16, 128]), ("BKT", [32, 64, 16, 128]), ("PC", [32, 64, 16])):
            CH[nm] = [K.dram(shp, F32, f"ch{nm}{b}") for b in range(NB)]
        for nm in ("At", "Bh", "Kh", "V"):
            CH[nm] = [K.dram([S, 1024], F32, f"ch{nm}{b}") for b in range(NB)]
        ones = lambda t: K.op("pool", lambda e: e.memset(t, 1.0), wr=[])
        tribd = K.sb([128, 128], F32, "tribd")
        K.op("pool", lambda e: e.memset(tribd[:, :], 1.0), wr=[tribd])
        K.op("pool", lambda e: e.affine_select(out=tribd[:, :], in_=tribd[:, :], pattern=[[1, 128]], compare_op=ALU.is_ge, fill=0.0,
                                                 base=0, channel_multiplier=-1), rd=[tribd], wr=[tribd])
        K.op("pool", lambda e: e.memset(tribd[0:64, 64:128], 0.0), wr=[tribd])
        bones = K.sb([128, 128], F32, "bones")
        K.op("pool", lambda e: e.memset(bones[:, :], 0.0), wr=[bones])
        K.op("pool", lambda e: e.memset(bones[0:64, 0:64], 1.0), wr=[bones])
        K.op("pool", lambda e: e.memset(bones[64:128, 64:128], 1.0), wr=[bones])
        cind = K.sb([128, 2], F32, "cind")
        K.op("pool", lambda e: e.memset(cind[:, :], 0.0), wr=[cind])
        K.op("pool", lambda e: e.memset(cind[0:64, 0:1], 1.0), wr=[cind])
        K.op("pool", lambda e: e.memset(cind[64:128, 1:2], 1.0), wr=[cind])
        CH.update(tribd=tribd, bones=bones, cind=cind)
        for nm in ("cbs", "e0", "e1", "em", "Bt", "eC", "Bh"):
            CH["s_" + nm] = K.sb([128, 1024], F32, "c" + nm)
        CH["XTs"] = K.sb([64, 2, 16, 2, 64], F32, "XTs")
        CH["BKs"] = K.sb([64, 2, 16, 2, 64], F32, "BKs")
        CH["PCt"] = K.sb([64, 2, 16], F32, "PCt")
        return CH

    def rwkv_chunk_prep(self, b, tile, rs, wk, rkv, CH):
        K = self.K
        ld = wk["dec"]
        K.op("dve", lambda e: e.tensor_scalar_mul(out=ld[:, :], in0=wk["t1"][:, :], scalar1=-0.6065306597126334), rd=[wk["t1"]], wr=[ld])
        cbs, e0, e1, em, Bt, eC, Bh = (CH["s_" + n] for n in ("cbs", "e0", "e1", "em", "Bt", "eC", "Bh"))
        for half in range(2):
            hs = slice(half * 512, (half + 1) * 512)
            cb, cc = self.pb[2 + half], self.pb[4 + half]
            K.op("pe", lambda e: e.matmul(cb[:, :], lhsT=CH["tribd"][:, :], rhs=ld[:, hs], start=True, stop=True), rd=[CH["tribd"], ld], wr=[cb])
            K.op("pe", lambda e: e.matmul(cc[:, :], lhsT=CH["bones"][:, :], rhs=ld[:, hs], start=True, stop=True), rd=[CH["bones"], ld], wr=[cc])
            K.op("act", lambda e: e.activation(out=cbs[:, hs], in_=cb[:, :], func=AF.Copy), rd=[cb], wr=[cbs])
            K.op("dve", lambda e: e.tensor_tensor(out=eC[:, hs], in0=cc[:, :], in1=cbs[:, hs], op=ALU.subtract), rd=[cc, cbs], wr=[eC])
        K.op("dve", lambda e: e.tensor_tensor(out=e0[:, :], in0=cbs[:, :], in1=ld[:, :], op=ALU.subtract), rd=[cbs, ld], wr=[e0])
        K.op("act", lambda e: e.activation(out=e0[:, :], in_=e0[:, :], func=AF.Exp), rd=[e0], wr=[e0])
        K.op("act", lambda e: e.activation(out=e1[:, :], in_=cbs[:, :], func=AF.Exp), rd=[cbs], wr=[e1])
        K.op("act", lambda e: e.activation(out=em[:, :], in_=cbs[:, :], func=AF.Exp, scale=-1.0), rd=[cbs], wr=[em])
        K.op("act", lambda e: e.activation(out=eC[:, :], in_=eC[:, :], func=AF.Exp), rd=[eC], wr=[eC])
        r_, v_ = rkv[:, 0:1024], rkv[:, 2048:3072]
        mul = lambda o, a, a_b, c, c_b: K.op("dve", lambda e: e.tensor_tensor(out=o[:, :], in0=a, in1=c, op=ALU.mult), rd=[a_b, c_b], wr=[o])
        mul(e0, wk["kk"][:, :], wk["kk"], e0[:, :], e0)
        mul(e1, r_, rkv, e1[:, :], e1)
        mul(Bt, wk["nb"][:, :], wk["nb"], em[:, :], em)
        mul(em, wk["k2"][:, :], wk["k2"], em[:, :], em)
        mul(Bh, wk["nb"][:, :], wk["nb"], eC[:, :], eC)
        mul(eC, wk["k2"][:, :], wk["k2"], eC[:, :], eC)
        pcb = self.pb[6]
        for h in range(16):
            K.op("pe", lambda e: e.matmul(pcb[0:64, h * 2:(h + 1) * 2], lhsT=ld[:, h * 64:(h + 1) * 64], rhs=CH["cind"][:, :],
                                          start=True, stop=True), rd=[ld, CH["cind"]], wr=[pcb])
        K.op("act", lambda e: e.activation(out=CH["PCt"][:, :, :].rearrange("p c h -> p h c"),
                                           in_=pcb[0:64, 0:32].rearrange("p (h c) -> p h c", c=2), func=AF.Exp), rd=[pcb], wr=[CH["PCt"]])
        for ci in range(2):
            K.dma("sp", CH["PC"][b][2 * tile + ci], CH["PCt"][:, ci, :], rd=[CH["PCt"]], wr=[CH["PC"][b]])
        bi = 0
        for stg, qs in ((CH["XTs"], ((e0, 0), (e1, 1))), (CH["BKs"], ((Bt, 0), (em, 1)))):
            for src, q in qs:
                for hg in range(4):
                    bank = self.pb[bi % 2]
                    bi += 1
                    for j in range(4):
                        h = hg * 4 + j
                        K.op("pe", lambda e: e.transpose(out=bank[0:64, j * 128:(j + 1) * 128], in_=src[:, h * 64:(h + 1) * 64],
                                                         identity=self.ident[:, :]), rd=[src, self.ident], wr=[bank])
                    dst = stg[:, :, hg * 4:(hg + 1) * 4, q, :].rearrange("p c h t -> p h c t")
                    srcv = bank[0:64, :].rearrange("p (h c t) -> p h c t", c=2, t=64)
                    if hg % 2 == 0:
                        K.op("act", lambda e: e.activation(out=dst, in_=srcv, func=AF.Copy), rd=[bank], wr=[stg])
                    else:
                        K.op("dve", lambda e: e.tensor_copy(out=dst, in_=srcv), rd=[bank], wr=[stg])
            dn = CH["XT"] if stg is CH["XTs"] else CH["BKT"]
            for ci in range(2):
                K.dma("sp", dn[b][2 * tile + ci], stg[:, ci, :, :, :].rearrange("p h q t -> p h (q t)"), rd=[stg], wr=[dn[b]])
        K.dma("sp", CH["At"][b][rs, :], e0[:, :], rd=[e0], wr=[CH["At"][b]])
        K.dma("sp", CH["Bh"][b][rs, :], Bh[:, :], rd=[Bh], wr=[CH["Bh"][b]])
        K.dma("sp", CH["Kh"][b][rs, :], eC[:, :], rd=[eC], wr=[CH["Kh"][b]])
        K.dma("sp", CH["V"][b][rs, :], v_, rd=[rkv], wr=[CH["V"][b]])

    def rwkv_chunks(self, CH, y_d):
        K = self.K
        with K.scope():
            sel = lambda t, pat, base, cm, rdwr: K.op("pool", lambda e: e.affine_select(
                out=t, in_=t, pattern=pat, compare_op=ALU.is_ge, fill=0.0, base=base, channel_multiplier=cm), rd=[rdwr], wr=[rdwr])
            gm = K.sb([128, 2, 64], F32, "gm")
            tb_ = K.sb([128, 2, 64], F32, "gmb")
            K.op("pool", lambda e: e.memset(gm[:, :, :], 1.0), wr=[gm])
            K.op("pool", lambda e: e.memset(tb_[:, :, :], 1.0), wr=[tb_])
            sel(gm[:, 0, :], [[1, 64]], -1, -1, gm)
            sel(gm[:, 1, :], [[1, 64]], 0, -1, gm)
            sel(tb_[:, 0, :], [[1, 64]], 63, -1, tb_)
            sel(tb_[:, 1, :], [[1, 64]], 64, -1, tb_)
            sel(tb_[:, :, :], [[0, 2], [0, 64]], -64, 1, tb_)
            K.op("dve", lambda e: e.tensor_tensor(out=gm[:, :, :], in0=gm[:, :, :], in1=tb_[:, :, :], op=ALU.add), rd=[gm, tb_], wr=[gm])
            lm = K.sb([64, 64], F32, "lm")
            K.op("pool", lambda e: e.memset(lm[:, :], 1.0), wr=[lm])
            sel(lm[:, :], [[-1, 64]], -1, 1, lm)
            id64 = self.ident[0:64, 0:64]
            dbl = lambda shp, nm: [K.sb(shp, F32, nm) for _ in range(2)]
            XT = dbl([64, 16, 128], "XT")
            PC = dbl([64, 16], "PC")
            CD = BF16 if self.cfg.get("ch_bf16", True) else F32
            Gs_ = [K.sb([128, 16, 128], CD, "Gs") for _ in range(2)]
            Lp_ = [[K.sb([64, 16, 64], CD, "Lp") for _ in range(2)] for _ in range(2)]
            LT_ = [[K.sb([64, 16, 64], CD, "LT") for _ in range(2)] for _ in range(2)]
            Xt_ = [K.sb([64, 16, 64], CD, "Xt") for _ in range(2)]
            Ab_ = [K.sb([64, 16, 64], CD, "Ab") for _ in range(2)]
            dblc = lambda shp, nm: [K.sb(shp, CD, nm) for _ in range(2)]
            XTc, BKTc, AWc = dblc([64, 16, 128], "XTc"), dblc([64, 16, 128], "BKTc"), dblc([64, 16, 128], "AWc")
            BKhc, UVc, Vpc = dblc([128, 16, 64], "BKhc"), dblc([128, 16, 64], "UVc"), dblc([128, 16, 64], "Vpc")
            for t in Vpc:
                K.op("pool", lambda e: e.memset(t[0:64, :, :], 0.0), wr=[t])
            RbT, Yb, PhT, Mb = dbl([64, 16, 64], "RbT"), dbl([64, 16, 64], "Yb"), dbl([64, 16, 64], "PhT"), dbl([64, 16, 64], "Mb")
            M = dbl([64, 16, 64], "M")
            self._M = [M, dbl([64, 16, 64], "M1")]
            yo = dbl([64, 16, 64], "yo")
            dg_ = [K.sb([64, 16, 64], F32, "dg") for _ in range(2)]
            for t in self._M[0] + self._M[1]:
                K.op("dve", lambda e: e.memset(t[:, :, :], 0.0), wr=[t])
            self._bk = 0

            def heads(w, rows, mk, rd):
                hpb = 512 // w
                for gi in range(16 // hpb):
                    bank = self.pb[self._bk % 8]
                    self._bk += 1
                    for hh in range(hpb):
                        h = gi * hpb + hh
                        l, r = mk(h)
                        if self.cfg.get("ch_f32r", False):
                            l, r = l.bitcast(mybir.dt.float32r), r.bitcast(mybir.dt.float32r)
                        K.op("pe", lambda e: e.matmul(bank[0:rows, hh * w:(hh + 1) * w], lhsT=l, rhs=r, start=True, stop=True), rd=rd, wr=[bank])
                    yield bank, bank[0:rows, :].rearrange("p (h w) -> p h w", w=w), slice(gi * hpb, (gi + 1) * hpb)

            ei = [0]

            def evac(bank, fn_act, fn_dve, rd, wr):
                ei[0] += 1
                if fn_act is not None and ei[0] % 2 == 0:
                    K.op("act", fn_act, rd=[bank] + rd, wr=wr)
                else:
                    K.op("dve", fn_dve, rd=[bank] + rd, wr=wr)

            for n in range(self.cfg.get('ch_n', 32)):
                cs = slice(n * 64, (n + 1) * 64)
                def pre(b, n=n, cs=cs):
                    Gs, Lp, LT, Xt, Ab, dg = Gs_[b], Lp_[b], LT_[b], Xt_[b], Ab_[b], dg_[b]
                    xt32, pc = XT[b], PC[b]
                    xt, bkt, aw, bkh, uv, vp = XTc[b], BKTc[b], AWc[b], BKhc[b], UVc[b], Vpc[b]
                    v3 = lambda ap: ap.rearrange("p (h j) -> p h j", j=64)
                    K.dma("sp", xt32[:, :, :], CH["XT"][b][n], rd=[CH["XT"][b]], wr=[xt32])
                    K.dma("sp", pc[:, :], CH["PC"][b][n], rd=[CH["PC"][b]], wr=[pc])
                    K.dma("pool", bkt[:, :, :], CH["BKT"][b][n], rd=[CH["BKT"][b]], wr=[bkt])
                    K.dma("pool", aw[:, :, 0:64], v3(CH["At"][b][cs, :]), rd=[CH["At"][b]], wr=[aw])
                    K.dma("pool", bkh[0:64, :, :], v3(CH["Bh"][b][cs, :]), rd=[CH["Bh"][b]], wr=[bkh])
                    K.dma("pool", bkh[64:128, :, :], v3(CH["Kh"][b][cs, :]), rd=[CH["Kh"][b]], wr=[bkh])
                    K.dma("pool", uv[64:128, :, :], v3(CH["V"][b][cs, :]), rd=[CH["V"][b]], wr=[uv])
                    K.dma("pool", vp[64:128, :, :], v3(CH["V"][b][cs, :]), rd=[CH["V"][b]], wr=[vp])
                    K.op("pool", lambda e: e.tensor_copy(out=xt[:, :, :], in_=xt32[:, :, :]), rd=[xt32], wr=[xt])
                    for bank, v, hs in heads(128, 128, lambda h: (bkt[:, h, :], xt[:, h, :]), [bkt, xt]):
                        K.op("dve", lambda e: e.tensor_tensor(out=Gs[:, hs, :].rearrange("p h (q t) -> p h q t", q=2),
                                                              in0=v.rearrange("p h (q t) -> p h q t", q=2),
                                                              in1=gm[:, :, :].unsqueeze(1).to_broadcast([128, 4, 2, 64]), op=ALU.mult),
                             rd=[bank, gm], wr=[Gs])
                    yield
                    if self.cfg.get('ch_steps', 99) < 2:
                        return
                    for bank, v, hs in heads(64, 64, lambda h: (xt[:, h, 0:64], bkt[:, h, 0:64]), [xt, bkt]):
                        K.op("dve", lambda e: e.tensor_tensor(out=Lp[0][:, hs, :], in0=v, in1=lm[:, :].unsqueeze(1).to_broadcast([64, 8, 64]),
                                                              op=ALU.mult), rd=[bank, lm], wr=[Lp[0]])
                    K.op("act", lambda e: e.activation(out=LT[0][:, :, :], in_=Gs[0:64, :, 0:64], func=AF.Copy), rd=[Gs], wr=[LT[0]])
                    yield
                    if self.cfg.get('ch_steps', 99) < 3:
                        return
                    K.op("dve", lambda e: e.tensor_tensor(out=Xt[:, :, :], in0=LT[0][:, :, :], in1=id64.unsqueeze(1).to_broadcast([64, 16, 64]),
                                                          op=ALU.add), rd=[LT[0], self.ident], wr=[Xt])
                    cur = 0
                    for k in range(1, 6):
                        nxt = 1 - cur
                        lp, lt = Lp[cur], LT[cur]
                        for bank, v, hs in heads(64, 64, lambda h: (lt[:, h, :], lp[:, h, :]), [lt, lp]):
                            evac(bank, lambda e: e.activation(out=Lp[nxt][:, hs, :], in_=v, func=AF.Copy),
                                 lambda e: e.tensor_copy(out=Lp[nxt][:, hs, :], in_=v), [], [Lp[nxt]])
                        if k < 5:
                            for bank, v, hs in heads(64, 64, lambda h: (lp[:, h, :], lt[:, h, :]), [lt, lp]):
                                evac(bank, lambda e: e.activation(out=LT[nxt][:, hs, :], in_=v, func=AF.Copy),
                                     lambda e: e.tensor_copy(out=LT[nxt][:, hs, :], in_=v), [], [LT[nxt]])
                        for bank, v, hs in heads(64, 64, lambda h: (Lp[nxt][:, h, :], Xt[:, h, :]), [Lp[nxt], Xt]):
                            K.op("dve", lambda e: e.tensor_tensor(out=Xt[:, hs, :], in0=Xt[:, hs, :], in1=v, op=ALU.add), rd=[bank, Xt], wr=[Xt])
                        cur = nxt
                        yield
                    yield
                    if self.cfg.get('ch_steps', 99) < 4:
                        return
                    for bank, v, hs in heads(64, 64, lambda h: (Gs[:, h, 0:64], vp[:, h, :]), [Gs, vp]):
                        evac(bank, lambda e: e.activation(out=aw[:, hs, 64:128], in_=v, func=AF.Copy),
                             lambda e: e.tensor_copy(out=aw[:, hs, 64:128], in_=v), [], [aw])
                    yield
                    if self.cfg.get('ch_steps', 99) < 5:
                        return
                    for bank, v, hs in heads(128, 64, lambda h: (Xt[:, h, :], aw[:, h, :]), [Xt, aw]):
                        K.op("dve", lambda e: e.tensor_copy(out=Ab[:, hs, :], in_=v[:, :, 0:64]), rd=[bank], wr=[Ab])
                        K.op("dve", lambda e: e.tensor_copy(out=uv[0:64, hs, :], in_=v[:, :, 64:128]), rd=[bank], wr=[uv])
                    yield
                    if self.cfg.get('ch_steps', 99) < 6:
                        return
                    for bank, v, hs in heads(64, 64, lambda h: (Ab[:, h, :], Gs[0:64, h, 64:128]), [Ab, Gs]):
                        K.op("dve", lambda e: e.tensor_tensor(out=RbT[b][:, hs, :], in0=xt32[:, hs, 64:128], in1=v, op=ALU.add),
                             rd=[bank, xt32], wr=[RbT[b]])
                    yield
                    if self.cfg.get('ch_steps', 99) < 7:
                        return
                    for bank, v, hs in heads(64, 64, lambda h: (Gs[:, h, 64:128], uv[:, h, :]), [Gs, uv]):
                        evac(bank, lambda e: e.activation(out=Yb[b][:, hs, :], in_=v, func=AF.Copy),
                             lambda e: e.tensor_copy(out=Yb[b][:, hs, :], in_=v), [], [Yb[b]])
                    for bank, v, hs in heads(64, 64, lambda h: (bkh[:, h, :], uv[:, h, :]), [bkh, uv]):
                        evac(bank, lambda e: e.activation(out=Mb[b][:, hs, :], in_=v, func=AF.Copy),
                             lambda e: e.tensor_copy(out=Mb[b][:, hs, :], in_=v), [], [Mb[b]])
                    yield
                    if self.cfg.get('ch_steps', 99) < 9:
                        return
                    K.op("dve", lambda e: e.tensor_tensor(out=dg[:, :, :], in0=id64.unsqueeze(1).to_broadcast([64, 16, 64]),
                                                          in1=pc[:, :].unsqueeze(2).to_broadcast([64, 16, 64]), op=ALU.mult),
                         rd=[self.ident, pc], wr=[dg])
                    for bank, v, hs in heads(64, 64, lambda h: (Ab[:, h, :], bkh[0:64, h, :]), [Ab, bkh]):
                        K.op("dve", lambda e: e.tensor_tensor(out=PhT[b][:, hs, :], in0=dg[:, hs, :], in1=v, op=ALU.add),
                             rd=[bank, dg], wr=[PhT[b]])
                gens = [pre(b) for b in range(NB)]
                while gens:
                    for g_ in list(gens):
                        try:
                            next(g_)
                        except StopIteration:
                            gens.remove(g_)
                for b in range(NB if self.cfg.get('ch_steps', 99) >= 10 else 0):
                    m_old, m_new = M[n % 2], M[(n + 1) % 2]
                    m_old, m_new = self._M[b][n % 2], self._M[b][(n + 1) % 2]
                    y = yo[b]
                    for bank, v, hs in heads(64, 64, lambda h: (RbT[b][:, h, :], m_old[:, h, :]), [RbT[b], m_old]):
                        K.op("dve", lambda e: e.tensor_tensor(out=y[:, hs, :], in0=Yb[b][:, hs, :], in1=v, op=ALU.add), rd=[bank, Yb[b]], wr=[y])
                    for bank, v, hs in heads(64, 64, lambda h: (PhT[b][:, h, :], m_old[:, h, :]), [PhT[b], m_old]):
                        K.op("dve", lambda e: e.tensor_tensor(out=m_new[:, hs, :], in0=Mb[b][:, hs, :], in1=v, op=ALU.add),
                             rd=[bank, Mb[b]], wr=[m_new])
                    K.dma("sp", y_d[b].t.ap().rearrange("h t i -> t h i")[cs], y[:, :, :], rd=[y], wr=[y_d[b]])

    def rwkv(self, SC, mix):
        K = self.K
        P = self.rwkv_params
        T = 32
        scin = [K.dram([16, S, 5, 64], F32, f"scin{b}") for b in range(NB)]
        scv = [K.dram([16, S, 64], F32, f"scv{b}") for b in range(NB)]
        y_d = [K.dram([16, S, 64], F32, f"yscan{b}") for b in range(NB)]
        g_d = [K.dram([S, 1024], F32, f"gd{b}") for b in range(NB)]
        bo_d = [K.dram([S, 1024], F32, f"bod{b}") for b in range(NB)]
        rows_src = P["rowp"].t.ap().unsqueeze(0).to_broadcast([128, 7, 1024])
        with K.scope():
            rows = K.sb([128, 5, 1024], F32, "rows")
            K.dma("sp", rows[:, 0:4, :], P["rowp"].t.ap()[0:4].unsqueeze(0).to_broadcast([128, 4, 1024]), rd=[P["rowp"]], wr=[rows])
            K.dma("sp", rows[:, 4:5, :], P["rowp"].t.ap()[6:7].unsqueeze(0).to_broadcast([128, 1, 1024]), rd=[P["rowp"]], wr=[rows])
            wup = K.sb([128, 1024], F32, "wup")
            aup = K.sb([128, 1024], F32, "aup")
            gup = K.sb([128, 1024], F32, "gup")
            for dst, nm in ((wup, "wup"), (aup, "aup"), (gup, "gup")):
                K.dma("sp", dst[:, :], P[nm][:, :], rd=[P[nm]], wr=[dst])
            zt = [K.sb([128, 26, 128], F32, "zt") for _ in range(2)]
            rkv = K.sb([128, 3072], F32, "rkv")
            wk = {n: K.sb([128, 1024], F32, n) for n in ("dec", "a", "g", "kk", "k2", "nb", "t1", "t2", "bonus")}
            ss = K.sb([128, 16], F32, "ss")
            rk = K.sb([128, 16], F32, "rk")
            v3 = lambda ap: ap.rearrange("p (h j) -> p h j", j=64)
            dn = ("t1", "kk", "k2", "nb")
            wk_dbl = [{n: wk[n] for n in dn}, {n: K.sb([128, 1024], F32, n + "b") for n in dn}]
            rkvs = [rkv, K.sb([128, 3072], F32, "rkvb")]
            CH = self.rwkv_chunk_setup() if self.cfg.get("rwkv_chunked", True) else None
            items = []
            chunked = self.cfg.get("rwkv_chunked", True)
            wk_shared = wk
            for b in range(NB):
                for tile in range(16):
                    ti = b * 16 + tile
                    wk_i = dict(wk_shared, **wk_dbl[ti % 2]) if chunked else wk_shared
                    rkv_i = rkvs[ti % 2] if chunked else rkv

                    def stA(b=b, tile=tile, wk=wk_i, rkv=rkv_i):
                        z = zt[tile % 2]
                        rs = slice(tile * 128, (tile + 1) * 128)
                        K.dma("sp", z[:, :, :], SC["z0"][b].t.ap().rearrange("c p t -> p c t")[:, :, rs], rd=[SC["z0"][b]], wr=[z])
                        for cg in range(6):
                            bank = self.pb[cg % 2]
                            for j in range(4):
                                c = cg * 4 + j
                                K.op("pe", lambda e: e.transpose(out=bank[:, j * 128:(j + 1) * 128], in_=z[:, c, :],
                                                                 identity=self.ident[:, :]), rd=[z, self.ident], wr=[bank])
                            if cg % 2 == 0:
                                K.op("act", lambda e: e.activation(out=rkv[:, cg * 512:(cg + 1) * 512], in_=bank[:, :], func=AF.Copy),
                                     rd=[bank], wr=[rkv])
                            else:
                                K.op("dve", lambda e: e.tensor_copy(out=rkv[:, cg * 512:(cg + 1) * 512], in_=bank[:, :]),
                                     rd=[bank], wr=[rkv])
                        K.op("act", lambda e: e.activation(out=z[0:64, 24, :], in_=z[0:64, 24, :], func=AF.Tanh), rd=[z], wr=[z])
                        K.op("act", lambda e: e.activation(out=z[:, 25, :], in_=z[:, 25, :], func=AF.Sigmoid), rd=[z], wr=[z])
                        for half in range(2):
                            hs = slice(half * 512, (half + 1) * 512)
                            bw, ba, bg = self.pb[2 + half], self.pb[4 + half], self.pb[6 + half]
                            K.op("pe", lambda e: e.matmul(bw[:, :], lhsT=z[:, 24, :], rhs=wup[:, hs], start=True, stop=True), rd=[z, wup], wr=[bw])
                            K.op("pe", lambda e: e.matmul(ba[:, :], lhsT=z[:, 24, :], rhs=aup[:, hs], start=True, stop=True), rd=[z, aup], wr=[ba])
                            K.op("pe", lambda e: e.matmul(bg[:, :], lhsT=z[:, 25, :], rhs=gup[:, hs], start=True, stop=True), rd=[z, gup], wr=[bg])
                            K.op("dve", lambda e: e.tensor_tensor(out=wk["t1"][:, hs], in0=bw[:, :], in1=rows[:, 0, hs], op=ALU.add),
                                 rd=[bw, rows], wr=[wk["t1"]])
                            K.op("dve", lambda e: e.tensor_tensor(out=wk["a"][:, hs], in0=ba[:, :], in1=rows[:, 1, hs], op=ALU.add),
                                 rd=[ba, rows], wr=[wk["a"]])
                            K.op("act", lambda e: e.activation(out=wk["g"][:, hs], in_=bg[:, :], func=AF.Copy), rd=[bg], wr=[wk["g"]])
                        K.op("act", lambda e: e.activation(out=wk["t1"][:, :], in_=wk["t1"][:, :], func=AF.Sigmoid), rd=[wk["t1"]], wr=[wk["t1"]])
                        K.op("act", lambda e: e.activation(out=wk["a"][:, :], in_=wk["a"][:, :], func=AF.Sigmoid), rd=[wk["a"]], wr=[wk["a"]])
                        if not chunked:
                            K.op("act", lambda e: e.activation(out=wk["dec"][:, :], in_=wk["t1"][:, :], func=AF.Exp, scale=-0.6065306597126334),
                                 rd=[wk["t1"]], wr=[wk["dec"]])
                        r_, k_, v_ = rkv[:, 0:1024], rkv[:, 1024:2048], rkv[:, 2048:3072]
                        K.op("dve", lambda e: e.tensor_tensor(out=wk["kk"][:, :], in0=k_, in1=rows[:, 2, :], op=ALU.mult), rd=[rkv, rows], wr=[wk["kk"]])
                        K.op("dve", lambda e: e.tensor_tensor(out=wk["t2"][:, :], in0=wk["kk"][:, :], in1=wk["kk"][:, :], op=ALU.mult),
                             rd=[wk["kk"]], wr=[wk["t2"]])
                        K.op("dve", lambda e: e.tensor_reduce(out=ss[:, :], in_=v3(wk["t2"][:, :]), axis=AX.X, op=ALU.add), rd=[wk["t2"]], wr=[ss])
                        K.op("act", lambda e: e.activation(out=ss[:, :], in_=ss[:, :], func=AF.Sqrt), rd=[ss], wr=[ss])
                        K.op("dve", lambda e: e.tensor_scalar_max(out=ss[:, :], in0=ss[:, :], scalar1=1e-12), rd=[ss], wr=[ss])
                        K.op("dve", lambda e: e.reciprocal(out=ss[:, :], in_=ss[:, :]), rd=[ss], wr=[ss])
                        K.op("dve", lambda e: e.tensor_tensor(out=v3(wk["kk"][:, :]), in0=v3(wk["kk"][:, :]),
                                                              in1=ss[:, :].unsqueeze(2).to_broadcast([128, 16, 64]), op=ALU.mult),
                             rd=[wk["kk"], ss], wr=[wk["kk"]])
                        K.op("dve", lambda e: e.scalar_tensor_tensor(out=wk["t2"][:, :], in0=wk["a"][:, :], scalar=-1.0, in1=rows[:, 3, :],
                                                                    op0=ALU.add, op1=ALU.mult), rd=[wk["a"], rows], wr=[wk["t2"]])
                        K.op("dve", lambda e: e.scalar_tensor_tensor(out=wk["k2"][:, :], in0=wk["t2"][:, :], scalar=1.0, in1=k_,
                                                                    op0=ALU.add, op1=ALU.mult), rd=[wk["t2"], rkv], wr=[wk["k2"]])
                        K.op("dve", lambda e: e.scalar_tensor_tensor(out=wk["nb"][:, :], in0=wk["kk"][:, :], scalar=-1.0, in1=wk["a"][:, :],
                                                                    op0=ALU.mult, op1=ALU.mult), rd=[wk["kk"], wk["a"]], wr=[wk["nb"]])
                        K.op("dve", lambda e: e.tensor_tensor(out=wk["t2"][:, :], in0=r_, in1=wk["k2"][:, :], op=ALU.mult),
                             rd=[rkv, wk["k2"]], wr=[wk["t2"]])
                        K.op("dve", lambda e: e.tensor_tensor(out=wk["t2"][:, :], in0=wk["t2"][:, :], in1=rows[:, 4, :], op=ALU.mult),
                             rd=[wk["t2"], rows], wr=[wk["t2"]])
                        K.op("dve", lambda e: e.tensor_reduce(out=rk[:, :], in_=v3(wk["t2"][:, :]), axis=AX.X, op=ALU.add), rd=[wk["t2"]], wr=[rk])
                        K.op("dve", lambda e: e.tensor_tensor(out=v3(wk["bonus"][:, :]), in0=v3(v_),
                                                              in1=rk[:, :].unsqueeze(2).to_broadcast([128, 16, 64]), op=ALU.mult),
                             rd=[rkv, rk], wr=[wk["bonus"]])
                        K.dma("sp", g_d[b][rs, :], wk["g"][:, :], rd=[wk["g"]], wr=[g_d[b]])
                        K.dma("sp", bo_d[b][rs, :], wk["bonus"][:, :], rd=[wk["bonus"]], wr=[bo_d[b]])
                        if not chunked:
                            sv = scin[b].t.ap().rearrange("h t c j -> t c h j")
                            for ci, (buf, ap) in enumerate(((wk["kk"], wk["kk"][:, :]), (wk["dec"], wk["dec"][:, :]), (wk["nb"], wk["nb"][:, :]),
                                                            (wk["k2"], wk["k2"][:, :]), (rkv, r_))):
                                K.dma("sp", sv[rs, ci], v3(ap), rd=[buf], wr=[scin[b]])
                            K.dma("sp", scv[b].t.ap().rearrange("h t i -> t h i")[rs], v3(v_), rd=[rkv], wr=[scv[b]])

                    def stB(b=b, tile=tile, wk=wk_i, rkv=rkv_i):
                        self.drip_cast(2)
                        if chunked:
                            self.rwkv_chunk_prep(b, tile, slice(tile * 128, (tile + 1) * 128), wk, rkv, CH)
                    items.append((stA, stB))
            self.pipeline(items)
        if self.cfg.get("rwkv_chunked", True) and not self.cfg.get("ch_prep_only", False):
            self.rwkv_chunks(CH, y_d)
        with (K.scope() if not self.cfg.get("rwkv_chunked", True) else contextlib.nullcontext()):
          if not self.cfg.get("rwkv_chunked", True):
              St = K.sb([128, 16, 64], F32, "state")
              K.op("dve", lambda e: e.memset(St[:, :, :], 0.0), wr=[St])
              tmp = K.sb([128, 16, 64], F32, "tmp")
              kv = [K.sb([128, 16, 64], F32, "kv") for _ in range(2)]
              sa = K.sb([128, 16], F32, "sa")
              vec = [K.sb([128, T, 5, 64], F32, "vec") for _ in range(2)]
              vv = [K.sb([128, T, 16], F32, "vv") for _ in range(2)]
              yb = [K.sb([128, T, 16], F32, "yb") for _ in range(2)]
              bc_j = lambda ap: ap.unsqueeze(1).to_broadcast([128, 16, 64])
              bc_i = lambda ap: ap.unsqueeze(2).to_broadcast([128, 16, 64])
              for ch in range(S // T):
                  ve, vb_, yo = vec[ch % 2], vv[ch % 2], yb[ch % 2]
                  ts_ = slice(ch * T, (ch + 1) * T)
                  for b in range(NB):
                      for i4 in range(4):
                          ps_ = slice(b * 64 + i4, b * 64 + 64, 4)
                          K.dma("sp", ve[ps_, :, :, :], scin[b][:, ts_, :, :], rd=[scin[b]], wr=[ve])
                          K.dma("sp", vb_[ps_, :, :], scv[b][:, ts_, i4 * 16:(i4 + 1) * 16], rd=[scv[b]], wr=[vb_])
                  for tl in range(T):
                      kvb = kv[tl % 2]
                      K.op("pool", lambda e: e.tensor_tensor(out=kvb[:, :, :], in0=bc_j(ve[:, tl, 3, :]), in1=bc_i(vb_[:, tl, :]),
                                                             op=ALU.mult), rd=[ve, vb_], wr=[kvb])
                      K.op("dve", lambda e: e.tensor_tensor(out=tmp[:, :, :], in0=St[:, :, :], in1=bc_j(ve[:, tl, 0, :]), op=ALU.mult),
                           rd=[St, ve], wr=[tmp])
                      K.op("dve", lambda e: e.tensor_reduce(out=sa[:, :], in_=tmp[:, :, :], axis=AX.X, op=ALU.add), rd=[tmp], wr=[sa])
                      K.op("dve", lambda e: e.tensor_tensor(out=St[:, :, :], in0=St[:, :, :], in1=bc_j(ve[:, tl, 1, :]), op=ALU.mult),
                           rd=[St, ve], wr=[St])
                      K.op("dve", lambda e: e.tensor_tensor(out=tmp[:, :, :], in0=bc_j(ve[:, tl, 2, :]), in1=bc_i(sa[:, :]), op=ALU.mult),
                           rd=[sa, ve], wr=[tmp])
                      K.op("dve", lambda e: e.tensor_tensor(out=St[:, :, :], in0=St[:, :, :], in1=tmp[:, :, :], op=ALU.add),
                           rd=[St, tmp], wr=[St])
                      K.op("dve", lambda e: e.tensor_tensor(out=St[:, :, :], in0=St[:, :, :], in1=kvb[:, :, :], op=ALU.add),
                           rd=[St, kvb], wr=[St])
                      K.op("dve", lambda e: e.tensor_tensor(out=tmp[:, :, :], in0=St[:, :, :], in1=bc_j(ve[:, tl, 4, :]), op=ALU.mult),
                           rd=[St, ve], wr=[tmp])
                      K.op("dve", lambda e: e.tensor_reduce(out=yo[:, tl, :], in_=tmp[:, :, :], axis=AX.X, op=ALU.add), rd=[tmp], wr=[yo])
                  for b in range(NB):
                      for i4 in range(4):
                          ps_ = slice(b * 64 + i4, b * 64 + 64, 4)
                          K.dma("sp", y_d[b][:, ts_, i4 * 16:(i4 + 1) * 16], yo[ps_, :, :], rd=[yo], wr=[y_d[b]])
        with K.scope():
            rows = K.sb([128, 7, 1024], F32, "rows")
            K.dma("sp", rows[:, :, :], rows_src, rd=[P["rowp"]], wr=[rows])
            yt = [K.sb([128, 16, 64], F32, "yt") for _ in range(2)]
            gt = [K.sb([128, 1024], F32, "gt") for _ in range(2)]
            bt = [K.sb([128, 1024], F32, "bt") for _ in range(2)]
            sq = K.sb([128, 16, 64], F32, "sq")
            mu = K.sb([128, 16], F32, "mu")
            var = K.sb([128, 16], F32, "var")
            bc_i = lambda ap: ap.unsqueeze(2).to_broadcast([128, 16, 64])
            v3 = lambda ap: ap.rearrange("p (h j) -> p h j", j=64)
            i = 0
            for b in range(NB):
                for tile in range(16):
                    rs = slice(tile * 128, (tile + 1) * 128)
                    y, g, bo = yt[i % 2], gt[i % 2], bt[i % 2]
                    i += 1
                    K.dma("sp", y[:, :, :], y_d[b].t.ap().rearrange("h t i -> t h i")[rs], rd=[y_d[b]], wr=[y])
                    K.dma("sp", g[:, :], g_d[b][rs, :], rd=[g_d[b]], wr=[g])
                    K.dma("sp", bo[:, :], bo_d[b][rs, :], rd=[bo_d[b]], wr=[bo])
                    K.op("dve", lambda e: e.tensor_reduce(out=mu[:, :], in_=y[:, :, :], axis=AX.X, op=ALU.add), rd=[y], wr=[mu])
                    K.op("dve", lambda e: e.tensor_scalar_mul(out=mu[:, :], in0=mu[:, :], scalar1=1.0 / 64), rd=[mu], wr=[mu])
                    K.op("dve", lambda e: e.tensor_tensor(out=y[:, :, :], in0=y[:, :, :], in1=bc_i(mu[:, :]), op=ALU.subtract),
                         rd=[y, mu], wr=[y])
                    K.op("dve", lambda e: e.tensor_tensor(out=sq[:, :, :], in0=y[:, :, :], in1=y[:, :, :], op=ALU.mult), rd=[y], wr=[sq])
                    K.op("dve", lambda e: e.tensor_reduce(out=var[:, :], in_=sq[:, :, :], axis=AX.X, op=ALU.add), rd=[sq], wr=[var])
                    K.op("act", lambda e: e.activation(out=var[:, :], in_=var[:, :], func=AF.Sqrt, bias=6.4e-4, scale=1.0 / 64),
                         rd=[var], wr=[var])
                    K.op("dve", lambda e: e.reciprocal(out=var[:, :], in_=var[:, :]), rd=[var], wr=[var])
                    K.op("dve", lambda e: e.tensor_tensor(out=y[:, :, :], in0=y[:, :, :], in1=bc_i(var[:, :]), op=ALU.mult),
                         rd=[y, var], wr=[y])
                    K.op("dve", lambda e: e.tensor_tensor(out=y[:, :, :], in0=y[:, :, :], in1=v3(rows[:, 4, :]), op=ALU.mult),
                         rd=[y, rows], wr=[y])
                    K.op("dve", lambda e: e.tensor_tensor(out=y[:, :, :], in0=y[:, :, :], in1=v3(rows[:, 5, :]), op=ALU.add),
                         rd=[y, rows], wr=[y])
                    K.op("dve", lambda e: e.tensor_tensor(out=y[:, :, :], in0=y[:, :, :], in1=v3(bo[:, :]), op=ALU.add),
                         rd=[y, bo], wr=[y])
                    K.op("dve", lambda e: e.tensor_tensor(out=y[:, :, :], in0=y[:, :, :], in1=v3(g[:, :]), op=ALU.mult),
                         rd=[y, g], wr=[y])
                    K.dma("sp", v3(mix[b][rs, 1024:2048]), y[:, :, :], rd=[y], wr=[mix[b]])


    def inproj_odd(self, xT, gbuf, goff, W, SC):
        K = self.K
        with K.scope():
            nbs = self.norm_bufs2()
            blocks = [(b, tb) for b in range(NB) for tb in range(NTB)]
            wbufs = [K.sb([128, NC, 128], BF16, "wi") for _ in range(3)]
            wg = [K.sb([128, NC, 512], BF16, "wg") for _ in range(2)]
            st16 = [K.sb([128, TB], BF16, "st16") for _ in range(2)]
            rot = [K.sb([128, TB], F32, "rot") for _ in range(2)]
            tabs = [K.sb([128, 2, TB], F32, "tab") for _ in range(2)]
            tst = [K.sb([128, 512], F32, "tst") for _ in range(2)]
            tsb = [K.sb([128, 512], BF16, "tsb") for _ in range(2)]
            i = 0
            self.norm_to_hT(xT[0], 0, gbuf, goff, nbs[0])
            for b in range(NB):
                for tb in range(NTB):
                    tsl = slice(tb * TB, (tb + 1) * TB)
                    bi_ = b * NTB + tb
                    hT = nbs[bi_ % 2][3]
                    for qk in range(2):
                        for c in range(4):
                            tab = tabs[c % 2]
                            K.dma("sp", tab[:, :, :], W["rtab"][qk, c, :, :, tsl], rd=[W["rtab"]], wr=[tab])
                            bz = self.mm_fm(W["fm"], W["fm"][qk * 4 + c], wbufs, hT, i)
                            i += 1
                            r_ = rot[c % 2]
                            K.op("dve", lambda e: e.tensor_tensor(out=r_[:, :], in0=bz[:, :], in1=tab[:, 0, :], op=ALU.mult),
                                 rd=[bz, tab], wr=[r_])
                            bs = self.mm_fm(W["fm"], W["fm"][8 + qk * 4 + c], wbufs, hT, i)
                            i += 1
                            K.op("dve", lambda e: e.tensor_tensor(out=tab[:, 1, :], in0=bs[:, :], in1=tab[:, 1, :], op=ALU.mult),
                                 rd=[bs, tab], wr=[tab])
                            stg = st16[c % 2]
                            K.op("dve", lambda e: e.tensor_tensor(out=stg[:, :], in0=r_[:, :], in1=tab[:, 1, :], op=ALU.add),
                                 rd=[r_, tab], wr=[stg])
                            dst = SC["rq"][b] if qk == 0 else SC["rk"][b]
                            K.dma("sp", dst[c, :, tsl], stg[:, :], rd=[stg], wr=[dst])
                    for j in range(16):
                        bank = self.mm_fm(W["fm"], W["fm"][16 + j], wbufs, hT, i)
                        i += 1
                        stg = st16[j % 2]
                        K.op("act", lambda e: e.activation(out=stg[:, :], in_=bank[:, :], func=AF.Copy,
                                                           scale=(0.125 if j < 8 else 1.0)), rd=[bank], wr=[stg])
                        if j < 8:
                            dst, idx = SC["nq"][b], j
                        else:
                            dst, idx = SC[("kc", "vc", "ks", "kw")[(j - 8) // 2]][b], (j - 8) % 2
                        K.dma("sp", dst[idx, :, tsl], stg[:, :], rd=[stg], wr=[dst])
                    if bi_ + 1 < len(blocks):
                        self.norm_to_hT(xT[blocks[bi_ + 1][0]], blocks[bi_ + 1][1], gbuf, goff, nbs[(bi_ + 1) % 2])
                    for gi in range(6):
                        ncol = 512 if gi < 5 else 48
                        w = wg[gi % 2]
                        K.dma("sp", w[:, :, 0:ncol], W["tm"][:, :, gi * 512:gi * 512 + ncol], rd=[W["tm"]], wr=[w])
                        for tt in range(4):
                            bank = self.pb[4 + tt % 2]
                            for c in range(NC):
                                K.op("pe", lambda e: e.matmul(bank[:, 0:ncol], lhsT=hT[:, c, tt * 128:(tt + 1) * 128], rhs=w[:, c, 0:ncol],
                                                              start=(c == 0), stop=(c == NC - 1)), rd=[hT, w], wr=[bank])
                            rs = slice(tb * TB + tt * 128, tb * TB + tt * 128 + 128)
                            if gi in (0, 1):
                                o = tsb[tt % 2]
                                K.op("dve", lambda e: e.tensor_copy(out=o[:, :], in_=bank[:, :]), rd=[bank], wr=[o])
                                K.dma("sp", SC["rv"][b][rs, gi * 512:(gi + 1) * 512], o[:, :], rd=[o], wr=[SC["rv"][b]])
                            elif gi in (2, 3):
                                o = tst[tt % 2]
                                K.op("act", lambda e: e.activation(out=o[:, :], in_=bank[:, :], func=AF.Silu), rd=[bank], wr=[o])
                                K.dma("sp", SC["rg"][b][rs, (gi - 2) * 512:(gi - 1) * 512], o[:, :], rd=[o], wr=[SC["rg"][b]])
                            elif gi == 4:
                                o = tsb[tt % 2]
                                K.op("dve", lambda e: e.tensor_copy(out=o[:, :], in_=bank[:, :]), rd=[bank], wr=[o])
                                K.dma("sp", SC["vsw"][b][rs, :], o[:, :], rd=[o], wr=[SC["vsw"][b]])
                            else:
                                o = tst[tt % 2]
                                K.op("act", lambda e: e.activation(out=o[:, 0:48], in_=bank[:, 0:48], func=AF.Sigmoid), rd=[bank], wr=[o])
                                K.dma("sp", SC["ng"][b][rs, :], o[:, 0:48], rd=[o], wr=[SC["ng"][b]])

    def retention(self, SC, mix):
        K = self.K
        with K.scope():
            qc_ = K.sb([128, S], BF16, "qc")
            klo = K.sb([128, S], BF16, "klo")
            khi = K.sb([128, S], BF16, "khi")
            K.op("pool", lambda e: e.memset(klo[64:128, :], 0.0), wr=[klo])
            K.op("pool", lambda e: e.memset(khi[0:64, :], 0.0), wr=[khi])
            vt = K.sb([128, 16, 2, 128], BF16, "vt")
            pbuf = [K.sb([128, 16, TB], BF16, "pbuf") for _ in range(2)]
            o = K.sb([128, 4, 128], F32, "o")
            sq = K.sb([128, 4, 128], F32, "sq")
            gt = [K.sb([128, 4, 128], F32, "gt") for _ in range(2)]
            mu = K.sb([128, 4], F32, "mu")
            var = K.sb([128, 4], F32, "var")
            bc = lambda ap: ap.unsqueeze(2).to_broadcast([128, 4, 128])
            it = 0
            for b in range(NB):
                for c in range(4):
                    K.dma("sp", qc_[:, :], SC["rq"][b][c], rd=[SC["rq"][b]], wr=[qc_])
                    K.dma("sp", klo[0:64, :], SC["rk"][b][c, 0:64, :], rd=[SC["rk"][b]], wr=[klo])
                    K.dma("sp", khi[64:128, :], SC["rk"][b][c, 64:128, :], rd=[SC["rk"][b]], wr=[khi])
                    for h2 in range(2):
                        h = 2 * c + h2
                        vsrc = SC["rv"][b].t.ap().rearrange("(kt p) f -> p kt f", p=128)[:, :, h * 128:(h + 1) * 128]
                        K.dma("sp", vt[:, :, h2, :], vsrc, rd=[SC["rv"][b]], wr=[vt])
                    items = []
                    for qc in range(4):
                        for h2 in range(2):
                            pb_ = pbuf[it % 2]
                            g = gt[it % 2]
                            ob = self.pb[4 + (it % 2)]
                            it += 1

                            def stA(qc=qc, h2=h2, pb_=pb_, g=g, b=b, c=c):
                                h = 2 * c + h2
                                kk = klo if h2 == 0 else khi
                                qs = slice(qc * TB, (qc + 1) * TB)
                                gsrc = SC["rg"][b].t.ap().rearrange("(s p) f -> p s f", p=128)[:, qc * 4:(qc + 1) * 4, h * 128:(h + 1) * 128]
                                K.dma("sp", g[:, :, :], gsrc, rd=[SC["rg"][b]], wr=[g])
                                for kt in range(4 * qc + 4):
                                    bank = self.pb[kt % 2]
                                    K.op("pe", lambda e: e.matmul(bank[:, :], lhsT=kk[:, kt * 128:(kt + 1) * 128], rhs=qc_[:, qs],
                                                                  start=True, stop=True), rd=[kk, qc_], wr=[bank])
                                    if kt % 2 == 0:
                                        K.op("act", lambda e: e.activation(out=pb_[:, kt, :], in_=bank[:, :], func=AF.Copy), rd=[bank], wr=[pb_])
                                    else:
                                        K.op("dve", lambda e: e.tensor_copy(out=pb_[:, kt, :], in_=bank[:, :]), rd=[bank], wr=[pb_])
                                    s_ = kt - 4 * qc
                                    if s_ >= 0:
                                        K.op("pool", lambda e: e.tensor_tensor(out=pb_[:, kt, s_ * 128:(s_ + 1) * 128],
                                                                               in0=pb_[:, kt, s_ * 128:(s_ + 1) * 128], in1=self.tri[:, :],
                                                                               op=ALU.mult), rd=[pb_, self.tri], wr=[pb_])

                            def stB(qc=qc, h2=h2, pb_=pb_, g=g, ob=ob, b=b, c=c):
                                h = 2 * c + h2
                                obv = ob[:, :].rearrange("p (s e) -> p s e", e=128)
                                for s_ in range(4):
                                    last = 4 * qc + s_
                                    for kt in range(last + 1):
                                        K.op("pe", lambda e: e.matmul(obv[:, s_, :], lhsT=pb_[:, kt, s_ * 128:(s_ + 1) * 128],
                                                                      rhs=vt[:, kt, h2, :], start=(kt == 0), stop=(kt == last)),
                                             rd=[pb_, vt], wr=[ob])
                                K.op("dve", lambda e: e.tensor_reduce(out=mu[:, :], in_=obv, axis=AX.X, op=ALU.add), rd=[ob], wr=[mu])
                                K.op("dve", lambda e: e.tensor_scalar_mul(out=mu[:, :], in0=mu[:, :], scalar1=1.0 / 128), rd=[mu], wr=[mu])
                                K.op("dve", lambda e: e.tensor_tensor(out=o[:, :, :], in0=obv, in1=bc(mu[:, :]), op=ALU.subtract),
                                     rd=[ob, mu], wr=[o])
                                K.op("dve", lambda e: e.tensor_tensor(out=sq[:, :, :], in0=o[:, :, :], in1=o[:, :, :], op=ALU.mult), rd=[o], wr=[sq])
                                K.op("dve", lambda e: e.tensor_reduce(out=var[:, :], in_=sq[:, :, :], axis=AX.X, op=ALU.add), rd=[sq], wr=[var])
                                K.op("act", lambda e: e.activation(out=var[:, :], in_=var[:, :], func=AF.Sqrt, bias=1e-6, scale=1.0 / 128),
                                     rd=[var], wr=[var])
                                K.op("dve", lambda e: e.reciprocal(out=var[:, :], in_=var[:, :]), rd=[var], wr=[var])
                                K.op("dve", lambda e: e.tensor_tensor(out=o[:, :, :], in0=o[:, :, :], in1=bc(var[:, :]), op=ALU.mult),
                                     rd=[o, var], wr=[o])
                                K.op("dve", lambda e: e.tensor_tensor(out=o[:, :, :], in0=o[:, :, :], in1=g[:, :, :], op=ALU.mult),
                                     rd=[o, g], wr=[o])
                                dst = mix[b].t.ap().rearrange("(s p) f -> p s f", p=128)[:, qc * 4:(qc + 1) * 4, h * 128:(h + 1) * 128]
                                K.dma("sp", dst, o[:, :, :], rd=[o], wr=[mix[b]])
                            items.append((stA, stB))
                    self.pipeline(items)


    def nsa(self, SC, mix):
        K = self.K
        P = self.nsa_params
        BIG = 1.0e4
        ocmp_d = [K.dram([S, 1024], F32, f"ocmp{b}") for b in range(NB)]
        with K.scope():
            sel_ge = lambda t, pat, base, cm: K.op("pool", lambda e: e.affine_select(
                out=t, in_=t, pattern=pat, compare_op=ALU.is_ge, fill=0.0, base=base, channel_multiplier=cm), rd=[], wr=[])
            IndC = K.sb([32, 16, 128], BF16, "IndC")
            K.op("pool", lambda e: e.memset(IndC[:, :, :], 1.0), wr=[IndC])
            K.op("pool", lambda e: e.affine_select(out=IndC[:, :, :], in_=IndC[:, :, :], pattern=[[128, 16], [1, 128]],
                                                     compare_op=ALU.is_ge, fill=0.0, base=0, channel_multiplier=-64), rd=[IndC], wr=[IndC])
            K.op("pool", lambda e: e.affine_select(out=IndC[:, :, :], in_=IndC[:, :, :], pattern=[[-128, 16], [-1, 128]],
                                                     compare_op=ALU.is_ge, fill=0.0, base=63, channel_multiplier=64), rd=[IndC], wr=[IndC])
            ovl = K.sb([128, 32], F32, "ovl")
            K.op("pool", lambda e: e.memset(ovl[:, :], 1.0), wr=[ovl])
            K.op("pool", lambda e: e.affine_select(out=ovl[:, :], in_=ovl[:, :], pattern=[[-64, 32]], compare_op=ALU.is_ge, fill=0.0,
                                                     base=31, channel_multiplier=16), rd=[ovl], wr=[ovl])
            K.op("pool", lambda e: e.affine_select(out=ovl[:, :], in_=ovl[:, :], pattern=[[64, 32]], compare_op=ALU.is_ge, fill=0.0,
                                                     base=63, channel_multiplier=-16), rd=[ovl], wr=[ovl])
            cmask = K.sb([128, 16, 127], F32, "cmask")
            K.op("pool", lambda e: e.memset(cmask[:, :, :], 1.0), wr=[cmask])
            K.op("pool", lambda e: e.affine_select(out=cmask[:, :, :], in_=cmask[:, :, :], pattern=[[128, 16], [-16, 127]],
                                                     compare_op=ALU.is_ge, fill=0.0, base=-31, channel_multiplier=1), rd=[cmask], wr=[cmask])
            fut = K.sb([128, 16, 32], F32, "fut")
            K.op("pool", lambda e: e.memset(fut[:, :, :], 1.0), wr=[fut])
            K.op("pool", lambda e: e.affine_select(out=fut[:, :, :], in_=fut[:, :, :], pattern=[[-128, 16], [64, 32]],
                                                     compare_op=ALU.is_gt, fill=0.0, base=0, channel_multiplier=-1), rd=[fut], wr=[fut])
            frc = K.sb([128, 16, 32], F32, "frc")
            K.op("pool", lambda e: e.memset(frc[:, :, :], 1.0), wr=[frc])
            K.op("pool", lambda e: e.affine_select(out=frc[:, :, :], in_=frc[:, :, :], pattern=[[128, 16], [-64, 32]],
                                                     compare_op=ALU.is_ge, fill=0.0, base=0, channel_multiplier=1), rd=[frc], wr=[frc])
            K.op("pool", lambda e: e.affine_select(out=frc[:, :, :], in_=frc[:, :, :], pattern=[[-128, 16], [64, 32]],
                                                     compare_op=ALU.is_ge, fill=0.0, base=63, channel_multiplier=-1), rd=[frc], wr=[frc])
            K.op("pool", lambda e: e.memset(frc[:, :, 0:1], 1.0), wr=[frc])
            Am = frc
            Bc = K.sb([128, 16, 32], F32, "Bc")
            nfut = fut
            K.op("dve", lambda e: e.tensor_tensor(out=Bc[:, :, :], in0=frc[:, :, :], in1=fut[:, :, :], op=ALU.subtract), rd=[frc, fut], wr=[Bc])
            K.op("dve", lambda e: e.tensor_tensor(out=Am[:, :, :], in0=frc[:, :, :], in1=fut[:, :, :], op=ALU.add), rd=[frc, fut], wr=[Am])
            K.op("dve", lambda e: e.tensor_scalar(out=Am[:, :, :], in0=Am[:, :, :], scalar1=-1.0, scalar2=1.0, op0=ALU.mult, op1=ALU.add),
                 rd=[Am], wr=[Am])
            K.op("dve", lambda e: e.tensor_scalar_mul(out=Bc[:, :, :], in0=Bc[:, :, :], scalar1=BIG), rd=[Bc], wr=[Bc])
            K.op("dve", lambda e: e.tensor_scalar(out=nfut[:, :, :], in0=fut[:, :, :], scalar1=-1.0, scalar2=1.0, op0=ALU.mult, op1=ALU.add),
                 rd=[fut], wr=[nfut])
            tri2 = K.sb([128, 128], BF16, "tri2")
            K.op("dve", lambda e: e.tensor_scalar(out=tri2[:, :], in0=self.tri[:, :], scalar1=-1.0, scalar2=1.0, op0=ALU.mult, op1=ALU.add),
                 rd=[self.tri], wr=[tri2])
            w1 = {}
            w2p = K.sb([128, 2, 128], BF16, "w2p")
            K.dma("pool", w2p[:, :, :], P["w2kp"][:, :, :], rd=[P["w2kp"]], wr=[w2p])
            w2v = K.sb([128, 64], BF16, "w2v")
            K.dma("pool", w2v[:, :], P["w2v"][:, :], rd=[P["w2v"]], wr=[w2v])
            bias = {}
            for kv in ("k", "v"):
                w1[kv] = K.sb([64, 32, 128], BF16, "w1" + kv)
                K.dma("pool", w1[kv][:, :, :], P["w1" + kv][:, :, :], rd=[P["w1" + kv]], wr=[w1[kv]])
                pe = K.sb([64, 32], BF16, "pe" + kv)
                K.dma("pool", pe[:, :], P["pe" + kv][:, :], rd=[P["pe" + kv]], wr=[pe])
                bias[kv] = K.sb([128, 1], F32, "bias" + kv)
                bk = self.pb[0]
                for l in range(32):
                    K.op("pe", lambda e: e.matmul(bk[:, 0:1], lhsT=w1[kv][:, l, :], rhs=pe[:, l:l + 1], start=(l == 0), stop=(l == 31)),
                         rd=[w1[kv], pe], wr=[bk])
                K.op("act", lambda e: e.activation(out=bias[kv][:, :], in_=bk[:, 0:1], func=AF.Copy), rd=[bk], wr=[bias[kv]])
            zc = K.sb([64, S], BF16, "zc")
            gx = K.sb([128, 127], F32, "gx")
            gy = K.sb([128, 127], F32, "gy")
            hb = K.sb([128, 127], BF16, "hb")
            kclo = K.sb([128, 4, 127], BF16, "kclo")
            kchi = K.sb([128, 4, 127], BF16, "kchi")
            vcmp = K.sb([128, 4, 64], F32, "vcmp")
            qg = K.sb([128, 8, S], BF16, "qg")
            kb = {n: K.sb([128, S], BF16, n) for n in ("klo", "khi", "wlo", "whi")}
            for n in ("klo", "wlo"):
                K.op("pool", lambda e: e.memset(kb[n][64:128, :], 0.0), wr=[kb[n]])
            for n in ("khi", "whi"):
                K.op("pool", lambda e: e.memset(kb[n][0:64, :], 0.0), wr=[kb[n]])
            vsa = K.sb([128, 16, 65], BF16, "vsa")
            vwa = K.sb([128, 16, 65], BF16, "vwa")
            K.op("pool", lambda e: e.memset(vsa[:, :, 64:65], 1.0), wr=[vsa])
            K.op("pool", lambda e: e.memset(vwa[:, :, 64:65], 1.0), wr=[vwa])
            ee = K.sb([128, 16, 127], F32, "ee")
            den = K.sb([128, 16], F32, "den")
            pT = K.sb([128, 16, 128], F32, "pT")
            gtile = [K.sb([128, 48], F32, "gtile") for _ in range(2)]
            oc = [K.sb([128, 16, 64], F32, "oc")] * 2
            scs = K.sb([128, 4, 32], F32, "scs")
            cmp = K.sb([128, 4, 32, 32], BF16, "cmp")
            selb = K.sb([128, 16, 4, 32], F32, "selb")
            biasT_all = K.sb([32, 4, S], BF16, "biasT")
            pbuf = [K.sb([128, 16, TB], BF16, "pbuf") for _ in range(2)]
            pbw = [K.sb([128, 8, TB], BF16, "pbw") for _ in range(2)]
            gqs = [K.sb([128, 4, 48], F32, "gq")] * 2
            octs = [K.sb([128, 4, 128], F32, "oct")] * 2
            os_ = [K.sb([128, 4, 128], F32, "o")] * 2
            tmp = K.sb([128, 4, 64], F32, "tmpo")
            rec = K.sb([128, 2, 4], F32, "rec")
            it = 0
            for b in range(NB):
                for g in range(4):
                    for kv in ("k", "v"):
                        src = SC["kc" if kv == "k" else "vc"][b]
                        K.dma("sp", zc[:, :], src[g // 2, (g % 2) * 64:(g % 2) * 64 + 64, :], rd=[src], wr=[zc])
                        bk = self.pb[1]
                        for l in range(32):
                            K.op("pe", lambda e: e.matmul(bk[:, 0:127], lhsT=w1[kv][:, l, :], rhs=zc[:, l:l + 2017:16],
                                                          start=(l == 0), stop=(l == 31)), rd=[w1[kv], zc], wr=[bk])
                        K.op("act", lambda e: e.activation(out=gx[:, :], in_=bk[:, 0:127], func=AF.Identity, bias=bias[kv][:, 0:1]),
                             rd=[bk, bias[kv]], wr=[gx])
                        K.op("dve", lambda e: e.tensor_tensor(out=gy[:, :], in0=gx[:, :], in1=gx[:, :], op=ALU.mult), rd=[gx], wr=[gy])
                        K.op("dve", lambda e: e.tensor_scalar(out=gy[:, :], in0=gy[:, :], scalar1=0.044715, scalar2=1.0, op0=ALU.mult,
                                                              op1=ALU.add), rd=[gy], wr=[gy])
                        K.op("dve", lambda e: e.tensor_tensor(out=gy[:, :], in0=gy[:, :], in1=gx[:, :], op=ALU.mult), rd=[gy, gx], wr=[gy])
                        K.op("act", lambda e: e.activation(out=gy[:, :], in_=gy[:, :], func=AF.Tanh, scale=0.7978845608028654), rd=[gy], wr=[gy])
                        K.op("dve", lambda e: e.tensor_scalar(out=gy[:, :], in0=gy[:, :], scalar1=0.5, scalar2=0.5, op0=ALU.mult, op1=ALU.add),
                             rd=[gy], wr=[gy])
                        K.op("dve", lambda e: e.tensor_tensor(out=hb[:, :], in0=gy[:, :], in1=gx[:, :], op=ALU.mult), rd=[gy, gx], wr=[hb])
                        if kv == "k":
                            for half, dstb in ((0, kclo), (1, kchi)):
                                b2 = self.pb[2 + half]
                                K.op("pe", lambda e: e.matmul(b2[:, 0:127], lhsT=w2p[:, half, :], rhs=hb[:, :], start=True, stop=True),
                                     rd=[w2p, hb], wr=[b2])
                                K.op("act", lambda e: e.activation(out=dstb[:, g, :], in_=b2[:, 0:127], func=AF.Copy), rd=[b2], wr=[dstb])
                        else:
                            b2 = self.pb[2]
                            K.op("pe", lambda e: e.matmul(b2[0:127, 0:64], lhsT=hb[:, :], rhs=w2v[:, :], start=True, stop=True),
                                 rd=[hb, w2v], wr=[b2])
                            K.op("act", lambda e: e.activation(out=vcmp[0:127, g, :], in_=b2[0:127, 0:64], func=AF.Copy), rd=[b2], wr=[vcmp])
                K.dma("sp", qg[:, :, :], SC["nq"][b].t.ap().rearrange("c p t -> p c t"), rd=[SC["nq"][b]], wr=[qg])
                for tile in range(16):
                    rs = slice(tile * 128, (tile + 1) * 128)
                    gtl = gtile[tile % 2]
                    K.dma("sp", gtl[:, :], SC["ng"][b][rs, :], rd=[SC["ng"][b]], wr=[gtl])
                    for g in range(4):
                        bs = self.pb[g]
                        for r in range(4):
                            K.op("pe", lambda e: e.matmul(bs[:, r * 127:(r + 1) * 127], lhsT=qg[:, 2 * g + r // 2, rs],
                                                          rhs=(kclo if r % 2 == 0 else kchi)[:, g, :], start=True, stop=True),
                                 rd=[qg, kclo, kchi], wr=[bs])
                        K.op("act", lambda e: e.activation(out=ee[:, 4 * g:4 * g + 4, :], in_=bs[:, 0:508].rearrange("p (r c) -> p r c", c=127),
                                                           func=AF.Exp), rd=[bs], wr=[ee])
                    K.op("dve", lambda e: e.tensor_tensor(out=ee[:, :, :], in0=ee[:, :, :],
                                                          in1=cmask[:, tile, :].unsqueeze(1).to_broadcast([128, 16, 127]), op=ALU.mult),
                         rd=[ee, cmask], wr=[ee])
                    K.op("dve", lambda e: e.tensor_reduce(out=den[:, :], in_=ee[:, :, :], axis=AX.X, op=ALU.add), rd=[ee], wr=[den])
                    K.op("dve", lambda e: e.tensor_scalar_max(out=den[:, :], in0=den[:, :], scalar1=1e-30), rd=[den], wr=[den])
                    K.op("dve", lambda e: e.reciprocal(out=den[:, :], in_=den[:, :]), rd=[den], wr=[den])
                    K.op("dve", lambda e: e.tensor_tensor(out=ee[:, :, :], in0=ee[:, :, :],
                                                          in1=den[:, :].unsqueeze(2).to_broadcast([128, 16, 127]), op=ALU.mult),
                         rd=[ee, den], wr=[ee])
                    for g in range(4):
                        bt_ = self.pb[4 + g]
                        for r in range(4):
                            K.op("pe", lambda e: e.transpose(out=bt_[0:127, r * 128:(r + 1) * 128], in_=ee[:, 4 * g + r, :], identity=self.ident[:, :]),
                                 rd=[ee, self.ident], wr=[bt_])
                        K.op("act", lambda e: e.activation(out=pT[0:127, 4 * g:4 * g + 4, :], in_=bt_[0:127, :].rearrange("p (r t) -> p r t", r=4),
                                                           func=AF.Copy), rd=[bt_], wr=[pT])
                    ocv = oc[tile % 2]
                    bp = self.pb[3]
                    for g in range(4):
                        bo = self.pb[g % 3]
                        for r in range(4):
                            K.op("pe", lambda e: e.matmul(bo[:, r * 64:(r + 1) * 64], lhsT=pT[0:127, 4 * g + r, :], rhs=vcmp[0:127, g, :],
                                                          start=True, stop=True), rd=[pT, vcmp], wr=[bo])
                        K.op("dve", lambda e: e.tensor_tensor(out=ocv[:, 4 * g:4 * g + 4, :], in0=bo[:, 0:256].rearrange("p (r d) -> p r d", d=64),
                                                              in1=gtl[:, 12 * g:12 * g + 12].rearrange("p (r k) -> p r k", k=3)[:, :, 0:1].to_broadcast([128, 4, 64]),
                                                              op=ALU.mult), rd=[bo, gtl], wr=[ocv])
                        for r in range(4):
                            K.op("pe", lambda e: e.matmul(bp[:, g * 32:(g + 1) * 32], lhsT=pT[0:127, 4 * g + r, :], rhs=ovl[0:127, :],
                                                          start=(r == 0), stop=(r == 3)), rd=[pT, ovl], wr=[bp])
                    K.dma("sp", ocmp_d[b][rs, :].rearrange("p (r d) -> p r d", d=64), ocv[:, :, :], rd=[ocv], wr=[ocmp_d[b]])
                    bc4 = lambda ap: ap.unsqueeze(1).to_broadcast([128, 4, 32])
                    K.op("dve", lambda e: e.tensor_tensor(out=scs[:, :, :], in0=bp[:, 0:128].rearrange("p (g j) -> p g j", j=32),
                                                          in1=bc4(Am[:, tile, :]), op=ALU.mult), rd=[bp, Am], wr=[scs])
                    K.op("dve", lambda e: e.tensor_tensor(out=scs[:, :, :], in0=scs[:, :, :], in1=bc4(Bc[:, tile, :]), op=ALU.add), rd=[scs, Bc], wr=[scs])
                    K.op("dve", lambda e: e.tensor_tensor(out=cmp[:, :, :, :], in0=scs[:, :, :].unsqueeze(2).to_broadcast([128, 4, 32, 32]),
                                                          in1=scs[:, :, :].unsqueeze(3).to_broadcast([128, 4, 32, 32]), op=ALU.is_gt), rd=[scs], wr=[cmp])
                    K.op("dve", lambda e: e.tensor_reduce(out=selb[:, tile, :, :], in_=cmp[:, :, :, :], axis=AX.X, op=ALU.add), rd=[cmp], wr=[selb])
                    K.op("dve", lambda e: e.tensor_single_scalar(out=selb[:, tile, :, :], in_=selb[:, tile, :, :], scalar=15.5, op=ALU.is_lt),
                         rd=[selb], wr=[selb])
                    K.op("dve", lambda e: e.tensor_tensor(out=selb[:, tile, :, :], in0=selb[:, tile, :, :], in1=bc4(nfut[:, tile, :]), op=ALU.mult),
                         rd=[selb, nfut], wr=[selb])
                    K.op("dve", lambda e: e.tensor_scalar(out=selb[:, tile, :, :], in0=selb[:, tile, :, :], scalar1=-1.0, scalar2=30000.0,
                                                          op0=ALU.add, op1=ALU.mult), rd=[selb], wr=[selb])
                for g in range(4):
                    for tg in range(4):
                        bank = self.pb[tg % 2]
                        for j in range(4):
                            t = tg * 4 + j
                            K.op("pe", lambda e: e.transpose(out=bank[0:32, j * 128:(j + 1) * 128], in_=selb[:, t, g, :], identity=self.ident[:, :]),
                                 rd=[selb, self.ident], wr=[bank])
                        K.op("act", lambda e: e.activation(out=biasT_all[:, g, tg * 512:(tg + 1) * 512], in_=bank[0:32, :], func=AF.Copy),
                             rd=[bank], wr=[biasT_all])
                for g in range(4):
                    biasT = _Sub(biasT_all, g)
                    for nm, srcn in (("klo", "ks"), ("khi", "ks"), ("wlo", "kw"), ("whi", "kw")):
                        lo = 0 if nm.endswith("lo") else 64
                        K.dma("sp", kb[nm][lo:lo + 64, :], SC[srcn][b][g // 2, (g % 2) * 64:(g % 2) * 64 + 64, :], rd=[SC[srcn][b]], wr=[kb[nm]])
                    vv_ = SC["vsw"][b].t.ap().rearrange("(kt p) f -> p kt f", p=128)
                    K.dma("sp", vsa[:, :, 0:64], vv_[:, :, g * 64:(g + 1) * 64], rd=[SC["vsw"][b]], wr=[vsa])
                    K.dma("sp", vwa[:, :, 0:64], vv_[:, :, 256 + g * 64:256 + (g + 1) * 64], rd=[SC["vsw"][b]], wr=[vwa])
                    for c2 in range(2):
                        c = 2 * g + c2
                        items = []
                        for qc in range(4):
                            for h2 in range(2):
                                pb_, pw_ = pbuf[it % 2], pbw[it % 2]
                                ob1, ob2 = self.pb[4 + 2 * (it % 2)], self.pb[5 + 2 * (it % 2)]
                                it += 1

                                def stA(qc=qc, h2=h2, pb_=pb_, pw_=pw_, b=b, c=c, biasT=biasT):
                                    qs = slice(qc * TB, (qc + 1) * TB)
                                    ks_ = kb["klo"] if h2 == 0 else kb["khi"]
                                    kw_ = kb["wlo"] if h2 == 0 else kb["whi"]
                                    for kt in range(4 * qc + 4):
                                        bank = self.pb[kt % 2]
                                        K.op("pe", lambda e: e.matmul(bank[:, :], lhsT=ks_[:, kt * 128:(kt + 1) * 128], rhs=qg[:, c, qs],
                                                                      start=True, stop=False), rd=[ks_, qg], wr=[bank])
                                        K.op("pe", lambda e: e.matmul(bank[:, :], lhsT=IndC[:, kt, :], rhs=biasT[:, qs], start=False, stop=True),
                                             rd=[IndC, biasT], wr=[bank])
                                        K.op("act", lambda e: e.activation(out=pb_[:, kt, :], in_=bank[:, :], func=AF.Exp), rd=[bank], wr=[pb_])
                                        s_ = kt - 4 * qc
                                        if s_ >= 0:
                                            K.op("pool", lambda e: e.tensor_tensor(out=pb_[:, kt, s_ * 128:(s_ + 1) * 128],
                                                                                   in0=pb_[:, kt, s_ * 128:(s_ + 1) * 128],
                                                                                   in1=self.tri[:, :], op=ALU.mult), rd=[pb_, self.tri], wr=[pb_])
                                    kt0 = max(0, 4 * qc - 4)
                                    for kt in range(kt0, 4 * qc + 4):
                                        bank = self.pb[2 + kt % 2]
                                        ki = kt - kt0
                                        K.op("pe", lambda e: e.matmul(bank[:, :], lhsT=kw_[:, kt * 128:(kt + 1) * 128], rhs=qg[:, c, qs],
                                                                      start=True, stop=True), rd=[kw_, qg], wr=[bank])
                                        K.op("act", lambda e: e.activation(out=pw_[:, ki, :], in_=bank[:, :], func=AF.Exp), rd=[bank], wr=[pw_])
                                        for s_ in range(4):
                                            dlt = 4 * qc + s_ - kt
                                            if dlt == 0 or dlt == 4:
                                                m = self.tri if dlt == 0 else tri2
                                                K.op("pool", lambda e: e.tensor_tensor(out=pw_[:, ki, s_ * 128:(s_ + 1) * 128],
                                                                                       in0=pw_[:, ki, s_ * 128:(s_ + 1) * 128], in1=m[:, :],
                                                                                       op=ALU.mult), rd=[pw_, m], wr=[pw_])

                                def stB(qc=qc, h2=h2, pb_=pb_, pw_=pw_, ob1=ob1, ob2=ob2, b=b, c=c):
                                    h = 2 * c + h2
                                    gq, oct_, o = gqs[qc % 2], octs[qc % 2], os_[qc % 2]
                                    if h2 == 0:
                                        K.dma("sp", gq[:, :, :], SC["ng"][b].t.ap().rearrange("(s p) f -> p s f", p=128)[:, qc * 4:(qc + 1) * 4, :],
                                              rd=[SC["ng"][b]], wr=[gq])
                                        K.dma("sp", oct_[:, :, :],
                                              ocmp_d[b].t.ap().rearrange("(s p) f -> p s f", p=128)[:, qc * 4:(qc + 1) * 4, c * 128:(c + 1) * 128],
                                              rd=[ocmp_d[b]], wr=[oct_])
                                    kt0 = max(0, 4 * qc - 4)
                                    o1 = ob1[:, 0:260].rearrange("p (s e) -> p s e", e=65)
                                    for s_ in range(4):
                                        last = 4 * qc + s_
                                        for kt in range(last + 1):
                                            K.op("pe", lambda e: e.matmul(o1[:, s_, :], lhsT=pb_[:, kt, s_ * 128:(s_ + 1) * 128], rhs=vsa[:, kt, :],
                                                                          start=(kt == 0), stop=(kt == last)), rd=[pb_, vsa], wr=[ob1])
                                    o2 = ob2[:, 0:260].rearrange("p (s e) -> p s e", e=65)
                                    for s_ in range(4):
                                        qt = 4 * qc + s_
                                        first = max(0, qt - 4)
                                        for kt in range(first, qt + 1):
                                            K.op("pe", lambda e: e.matmul(o2[:, s_, :], lhsT=pw_[:, kt - kt0, s_ * 128:(s_ + 1) * 128], rhs=vwa[:, kt, :],
                                                                          start=(kt == first), stop=(kt == qt)), rd=[pw_, vwa], wr=[ob2])
                                    K.op("dve", lambda e: e.reciprocal(out=rec[:, 0, :], in_=o1[:, :, 64]), rd=[ob1], wr=[rec])
                                    K.op("dve", lambda e: e.reciprocal(out=rec[:, 1, :], in_=o2[:, :, 64]), rd=[ob2], wr=[rec])
                                    K.op("dve", lambda e: e.tensor_tensor(out=rec[:, :, :], in0=rec[:, :, :],
                                                                          in1=gq[:, :, 3 * h + 1:3 * h + 3].rearrange("p s k -> p k s"), op=ALU.mult),
                                         rd=[rec, gq], wr=[rec])
                                    od = o[:, :, h2 * 64:(h2 + 1) * 64]
                                    K.op("dve", lambda e: e.tensor_tensor(out=od, in0=o1[:, :, 0:64], in1=rec[:, 0, :].unsqueeze(2).to_broadcast([128, 4, 64]),
                                                                          op=ALU.mult), rd=[ob1, rec], wr=[o])
                                    K.op("dve", lambda e: e.tensor_tensor(out=tmp[:, :, :], in0=o2[:, :, 0:64],
                                                                          in1=rec[:, 1, :].unsqueeze(2).to_broadcast([128, 4, 64]), op=ALU.mult),
                                         rd=[ob2, rec], wr=[tmp])
                                    K.op("dve", lambda e: e.tensor_tensor(out=od, in0=od, in1=tmp[:, :, :], op=ALU.add), rd=[o, tmp], wr=[o])
                                    K.op("dve", lambda e: e.tensor_tensor(out=od, in0=od, in1=oct_[:, :, h2 * 64:(h2 + 1) * 64], op=ALU.add),
                                         rd=[o, oct_], wr=[o])
                                    if h2 == 1:
                                        dst = mix[b].t.ap().rearrange("(s p) f -> p s f", p=128)[:, qc * 4:(qc + 1) * 4, 1024 + c * 128:1024 + (c + 1) * 128]
                                        K.dma("sp", dst, o[:, :, :], rd=[o], wr=[mix[b]])
                                items.append((stA, stB))
                        self.pipeline(items)

def build(cfg):
    nc = bass.Bass("TRN2", target_bir_lowering=False)
    K = KB(nc)
    net = Net(K, cfg)
    phases = cfg.get("phases", ["moba", "rwkv", "ffn0", "ret", "nsa", "ffn1"])
    x = net.ext("x", [NB, S, D])
    out = K.dram([NB, S, D], F32, "out", kind="ExternalOutput")
    gin = net.ext("gcols", [128, 5 * NC])
    xT = [K.dram([NC, 128, S], F32, f"xT{b}") for b in range(NB)]
    mix = [K.dram([S, D], F32, f"mix{b}") for b in range(NB)]

    def castw(name, shape, defer=False):
        src = net.ext(name, shape)
        dst = K.dram(shape, BF16, name + "_bf")
        net.cast_dram(src, dst, shape, defer=defer)
        return dst

    net.consts()
    net.moba_consts()
    gsb = K.sb([128, 5 * NC], F32, "gsb")
    K.dma("sp", gsb[:, :], gin[:, :], rd=[gin], wr=[gsb])
    net.prologue_x(x, xT)

    if "moba" in phases or "rwkv" in phases:
        W = {"qkz": castw("e_qkz", [42, 128, NC, 128]), "v": castw("e_v", [128, NC, 1024]), "mu": net.ext("e_mu", [128, 26])}
        wout = castw("e_wout", [NC, 128, NC, 128])
        SC = {"q0": [K.dram([8, 128, S], BF16, f"q0_{b}") for b in range(NB)],
              "k0": [K.dram([8, 128, S], BF16, f"k0_{b}") for b in range(NB)],
              "v0": [K.dram([S, 1024], BF16, f"v0_{b}") for b in range(NB)],
              "z0": [K.dram([26, 128, S], F32, f"z0_{b}") for b in range(NB)]}
        net.rwkv_params = {"rowp": net.ext("e_rowp", [7, 1024]), "wup": net.ext("e_wup", [128, 1024]),
                           "aup": net.ext("e_aup", [128, 1024]), "gup": net.ext("e_gup", [128, 1024])}
        net.inproj_even(xT, gsb, 0, W, SC)
        if "moba" in phases:
            net.moba(SC, mix)
        if "ffn0" in phases:
            dfr = "rwkv" in phases
            w1b = castw("w1t", [2, NFC, 128, NC, 128], defer=dfr)
            w3b = castw("w3t", [2, NFC, 128, NC, 128], defer=dfr)
            w2b = castw("w2t", [2, NC, 128, NFC, 128], defer=dfr)
        if "rwkv" in phases:
            net.rwkv(SC, mix)
        net.drip_cast(1000)
        if "moba" in phases and "rwkv" in phases:
            net.outproj(xT, mix, wout)
        elif "moba" in phases:
            net.outproj(xT, mix, wout, nchunks=8)
        else:
            net.outproj(xT, mix, wout, nchunks=8, c0=8)
    if "ffn0" in phases:
        if not ("moba" in phases or "rwkv" in phases):
            w1b = castw("w1t", [2, NFC, 128, NC, 128])
            w3b = castw("w3t", [2, NFC, 128, NC, 128])
            w2b = castw("w2t", [2, NC, 128, NFC, 128])
    if "ret" in phases or "nsa" in phases:
        W = {"fm": castw("o_fm", [32, 128, NC, 128]), "tm": castw("o_tm", [128, NC, 2608 + 464]), "rtab": net.ext("o_rtab", [2, 4, 128, 2, S])}
        wout = castw("o_wout", [NC, 128, NC, 128])
    if "ffn0" in phases:
        net.ffn(0, xT, w1b, w3b, w2b, gsb, 2 * NC)
    if "ret" in phases or "nsa" in phases:
        SC = {}
        for nm, shp, dt in (("rq", [4, 128, S], BF16), ("rk", [4, 128, S], BF16), ("nq", [8, 128, S], BF16),
                            ("kc", [2, 128, S], BF16), ("vc", [2, 128, S], BF16), ("ks", [2, 128, S], BF16), ("kw", [2, 128, S], BF16),
                            ("rv", [S, 1024], BF16), ("rg", [S, 1024], F32), ("vsw", [S, 512], BF16), ("ng", [S, 48], F32)):
            SC[nm] = [K.dram(shp, dt, f"{nm}_{b}") for b in range(NB)]
        net.nsa_params = {"w1k": net.ext("n_w1k", [64, 32, 128]), "w1v": net.ext("n_w1v", [64, 32, 128]),
                          "w2kp": net.ext("n_w2kp", [128, 2, 128]), "w2v": net.ext("n_w2v", [128, 64]),
                          "pek": net.ext("n_pek", [64, 32]), "pev": net.ext("n_pev", [64, 32])}
        net.inproj_odd(xT, gsb, NC, W, SC)
        if "ret" in phases:
            net.retention(SC, mix)
        if "nsa" in phases:
            net.nsa(SC, mix)
        if "ret" in phases and "nsa" in phases:
            net.outproj(xT, mix, wout)
        elif "ret" in phases:
            net.outproj(xT, mix, wout, nchunks=8)
        else:
            net.outproj(xT, mix, wout, nchunks=8, c0=8)
    if "ffn1" in phases and "ffn0" not in phases:
        w1b = castw("w1t", [2, NFC, 128, NC, 128])
        w3b = castw("w3t", [2, NFC, 128, NC, 128])
        w2b = castw("w2t", [2, NC, 128, NFC, 128])
    if "ffn1" in phases:
        net.ffn(1, xT, w1b, w3b, w2b, gsb, 3 * NC)
    net.final(xT, gsb, 4 * NC, out)
    K.barrier()
    K.close()
    return nc, list(net.inp.keys())


def tile_rhs(w):
    Kd, N = w.shape
    return np.ascontiguousarray(w.reshape(Kd // 128, 128, N).transpose(1, 0, 2))


def host_inputs(inputs):
    f = lambda k: np.asarray(inputs[k], dtype=np.float32)
    sh = {}
    sh["gcols"] = np.concatenate([col_vec(f("mix_norm")[0]), col_vec(f("mix_norm")[1]), col_vec(f("ffn_norm")[0]),
                                  col_vec(f("ffn_norm")[1]), col_vec(f("final_norm"))], axis=1)
    sh["w1t"] = np.stack([tile_lhsT(f("ffn_w1")[l]) for l in range(2)])
    sh["w3t"] = np.stack([tile_lhsT(f("ffn_w3")[l]) for l in range(2)])
    sh["w2t"] = np.stack([tile_lhsT(f("ffn_w2")[l]) for l in range(2)])
    wi = f("even_w_in")[0]
    sh["e_qkz"] = tile_lhsT(np.concatenate([wi[:, 0:2048], wi[:, 3072:6400]], axis=1))
    sh["e_v"] = tile_rhs(wi[:, 2048:3072])
    sh["e_mu"] = col_vec(f("even_shift_mu")[0])
    sh["e_wout"] = tile_lhsT(f("even_w_out")[0])
    sh["e_rowp"] = np.stack([f("even_w0")[0], f("even_a0")[0], f("even_k_k")[0], f("even_k_a")[0], f("even_ln_g")[0],
                             f("even_ln_b")[0], f("even_r_k")[0].reshape(-1)])
    z64 = np.zeros((64, 1024), np.float32)
    sh["e_wup"] = np.concatenate([f("even_w_up")[0], z64], axis=0)
    sh["e_aup"] = np.concatenate([z64, f("even_a_up")[0]], axis=0)
    sh["e_gup"] = f("even_g_up")[0]
    wo = f("odd_w_in")[0]
    rq, rk = wo[:, 0:512], wo[:, 512:1024]
    swap = lambda w: w.reshape(D, 8, 2, 32)[:, :, ::-1, :].reshape(D, 512)
    fm = np.concatenate([rq, rk, swap(rq), swap(rk), wo[:, 3072:4096], wo[:, 4096:4352], wo[:, 4352:4608], wo[:, 4608:4864],
                         wo[:, 5120:5376]], axis=1)
    sh["o_fm"] = tile_lhsT(fm)
    tm = np.concatenate([wo[:, 1024:2048], wo[:, 2048:3072], wo[:, 4864:5120], wo[:, 5376:5632], wo[:, 5632:5680],
                         np.zeros((D, 464), np.float32)], axis=1)
    sh["o_tm"] = tile_rhs(tm)
    sh["o_rtab"] = rot_tables()
    sh["n_w1k"] = np.ascontiguousarray(f("odd_cmp_w1_k")[0].reshape(32, 64, 128).transpose(1, 0, 2))
    sh["n_w1v"] = np.ascontiguousarray(f("odd_cmp_w1_v")[0].reshape(32, 64, 128).transpose(1, 0, 2))
    w2k = f("odd_cmp_w2_k")[0]
    z = np.zeros_like(w2k)
    sh["n_w2kp"] = np.ascontiguousarray(np.stack([np.concatenate([w2k, z], 1), np.concatenate([z, w2k], 1)], axis=1))
    sh["n_w2v"] = f("odd_cmp_w2_v")[0]
    sh["n_pek"] = np.ascontiguousarray(f("odd_cmp_pe_k")[0].T)
    sh["n_pev"] = np.ascontiguousarray(f("odd_cmp_pe_v")[0].T)
    sh["o_wout"] = tile_lhsT(f("odd_w_out")[0])
    return sh


def rot_tables():
    t = np.arange(S, dtype=np.float64)
    inv = 10000.0 ** (-np.arange(32, dtype=np.float64) / 32)
    ang = t[None, :] * inv[:, None]
    cos = np.concatenate([np.cos(ang), np.cos(ang)], 0)
    sin = np.concatenate([-np.sin(ang), np.sin(ang)], 0)
    logg = np.log(1.0 - 2.0 ** (-5.0 - np.arange(8)))
    out = np.zeros((2, 4, 128, 2, S), np.float64)
    for h in range(8):
        c, h2 = h // 2, h % 2
        dq = np.exp(t * logg[h])[None, :]
        dk = np.exp(-t * logg[h])[None, :] * 0.125
        out[0, c, h2 * 64:(h2 + 1) * 64, 0] = cos * dq
        out[0, c, h2 * 64:(h2 + 1) * 64, 1] = sin * dq
        out[1, c, h2 * 64:(h2 + 1) * 64, 0] = cos * dk
        out[1, c, h2 * 64:(h2 + 1) * 64, 1] = sin * dk
    return out.astype(np.float32)


_CACHE = {}


def kernel(**inputs):
    cfg = inputs.pop("_cfg", {})
    ncores = cfg.get("ncores", 8)
    nc, names = build(cfg)
    shared = host_inputs(inputs)
    x = np.asarray(inputs["x"], dtype=np.float32)
    in_maps = []
    for i in range(ncores):
        m = {k: shared[k] for k in names if k != "x"}
        m["x"] = np.ascontiguousarray(x[i * NB:(i + 1) * NB])
        in_maps.append(m)
    res = run_bass_kernel_spmd(nc, in_maps, core_ids=list(range(ncores)))
    return np.concatenate([r["out"] for r in res.results], axis=0)
```
